# Optimizing a Trainium2 kernel written in Bass

```python
import math
import jax, jax.numpy as jnp
from jax import lax
import numpy as np

D_MODEL = 1024
BATCH = 2
SEQ = 16384
DEPTH = 2
DEC_BATCH = 4
DEC_SEQ = 4096
PAST_LEN = 128

GRID_W = 64
HEAD_DIM = 64
NA_HEADS = 4
NA_WIN_ROWS = 8
NA_WIN_COLS = 16
NA_QCOLS = 16
NA_KCOLS = 32
DA_HEADS = 4
DA_CONFIGS = ((128, 1), (512, 4), (2048, 16))
DA_QBLOCK = 64
N_BUCKETS = 32
MAX_DISTANCE = 1024
HG_HEADS = 4
HG_DK = 128
HG_DV = 128
HG_CHUNK = 64

NA_WIDTH = NA_HEADS * HEAD_DIM
DA_WIDTH = DA_HEADS * HEAD_DIM
HG_FDIM = HG_HEADS * HG_DK
HG_WIDTH = HG_HEADS * HG_DV
MIX_WIDTH = NA_WIDTH + DA_WIDTH + HG_WIDTH
IN_SPLITS = (NA_WIDTH,) * 3 + (DA_WIDTH,) * 3 + (HG_FDIM,) * 3 + (HG_WIDTH,) * 2
N_IN = sum(IN_SPLITS)
D_FF = 2816
NORM_EPS = 1e-6
NEG_INF = -1e30

kernel_name = "hybrid_na_dilated_hgrn2_encoder"


def rmsnorm(x, g):
    x32 = x.astype(jnp.float32)
    y = x32 * lax.rsqrt(jnp.mean(x32 * x32, axis=-1, keepdims=True) + NORM_EPS)
    return (y * g.astype(jnp.float32)).astype(x.dtype)


def t5_bucket_np(rel):
    nb = N_BUCKETS // 2
    max_exact = nb // 2
    ret = np.where(rel > 0, nb, 0)
    n = np.abs(rel)
    large = max_exact + (np.log(np.maximum(n, 1) / max_exact)
                         / np.log(MAX_DISTANCE / max_exact) * (nb - max_exact)).astype(np.int32)
    large = np.minimum(large, nb - 1)
    return (ret + np.where(n < max_exact, n, large)).astype(np.int32)


def neighbourhood_attention(q, k, v, rpb):
    B, T, H, hd = q.shape
    rows = T // GRID_W
    wr = min(NA_WIN_ROWS, rows)
    ncb = GRID_W // NA_QCOLS
    r = np.arange(rows)
    r0 = np.clip(r - wr // 2, 0, rows - wr)
    ri = r0[:, None] + np.arange(wr)[None, :]
    c = np.arange(GRID_W).reshape(ncb, NA_QCOLS)
    c0 = np.clip(c - NA_WIN_COLS // 2, 0, GRID_W - NA_WIN_COLS)
    kb0 = np.clip(np.arange(ncb) * NA_QCOLS - NA_WIN_COLS // 2, 0, GRID_W - NA_KCOLS)
    ci = kb0[:, None] + np.arange(NA_KCOLS)[None, :]
    col_ok = (ci[:, None, :] >= c0[:, :, None]) & (ci[:, None, :] < c0[:, :, None] + NA_WIN_COLS)
    dr = ri - r[:, None] + NA_WIN_ROWS - 1
    dc = np.clip(ci[:, None, :] - c[:, :, None], -(NA_WIN_COLS - 1), NA_WIN_COLS - 1) + NA_WIN_COLS - 1
    bias = rpb[:, dr[:, None, None, :, None], dc[None, :, :, None, :]].astype(jnp.float32)
    bias = jnp.where(col_ok[None, None, :, :, None, :], bias, NEG_INF)
    qg = q.reshape(B, rows, ncb, NA_QCOLS, H, hd)
    rsel = ri[:, :, None, None]
    csel = ci[None, None, :, :]
    kg = k.reshape(B, rows, GRID_W, H, hd)[:, rsel, csel]
    vg = v.reshape(B, rows, GRID_W, H, hd)[:, rsel, csel]
    s = jnp.einsum('brnqhd,brwnkhd->bhrnqwk', qg, kg).astype(jnp.float32) + bias[None]
    p = jax.nn.softmax(s.reshape(s.shape[:5] + (-1,)), axis=-1).reshape(s.shape).astype(v.dtype)
    o = jnp.einsum('bhrnqwk,brwnkhd->brnqhd', p, vg)
    return o.reshape(B, T, H * hd).astype(jnp.float32)


def dilated_branch(q, k, v, bias_table, window, dilation):
    B, T, H, hd = q.shape
    d = dilation
    L = T // d
    hw = window // (2 * d)
    nblk = -(-L // DA_QBLOCK)
    lp = nblk * DA_QBLOCK
    kb = DA_QBLOCK + 2 * hw

    def to_sub(x):
        return x.reshape(B, L, d, H, hd).transpose(0, 2, 3, 1, 4).reshape(B * d, H, L, hd)

    qs = jnp.pad(to_sub(q), ((0, 0), (0, 0), (0, lp - L), (0, 0))).reshape(B * d, H, nblk, DA_QBLOCK, hd)
    kidx = np.arange(nblk)[:, None] * DA_QBLOCK + np.arange(kb)[None, :]
    kpad = ((0, 0), (0, 0), (hw, lp - L + hw), (0, 0))
    ks = jnp.pad(to_sub(k), kpad)[:, :, kidx]
    vs = jnp.pad(to_sub(v), kpad)[:, :, kidx]
    rel = np.arange(kb)[None, :] - np.arange(DA_QBLOCK)[:, None] - hw
    pos = kidx - hw
    ok = (np.abs(rel) <= hw)[None] & ((pos >= 0) & (pos < L))[:, None, :]
    bias = bias_table[t5_bucket_np(rel * d)].transpose(2, 0, 1).astype(jnp.float32)
    s = jnp.einsum('zhcqd,zhckd->zhcqk', qs, ks).astype(jnp.float32) + bias[None, :, None]
    s = jnp.where(ok[None, None], s, NEG_INF)
    lse = jax.nn.logsumexp(s, axis=-1)
    p = jnp.exp(s - lse[..., None]).astype(v.dtype)
    o = jnp.einsum('zhcqk,zhckd->zhcqd', p, vs).reshape(B * d, H, lp, hd)[:, :, :L]
    o = o.reshape(B, d, H, L, hd).transpose(0, 3, 1, 2, 4).reshape(B, T, H, hd)
    lse = lse.reshape(B * d, H, lp)[:, :, :L].reshape(B, d, H, L).transpose(0, 3, 1, 2).reshape(B, T, H)
    return o.astype(jnp.float32), lse


def dilated_attention(q, k, v, bias_table):
    outs, lses = [], []
    for window, dilation in DA_CONFIGS:
        o, lse = dilated_branch(q, k, v, bias_table, window, dilation)
        outs.append(o)
        lses.append(lse)
    w = jax.nn.softmax(jnp.stack(lses), axis=0)
    o = jnp.sum(w[..., None] * jnp.stack(outs), axis=0)
    B, T = q.shape[:2]
    return o.reshape(B, T, DA_WIDTH)


def hgrn_lower_bounds(lb_logits):
    p = jax.nn.softmax(lb_logits.astype(jnp.float32), axis=1)
    c = jnp.cumsum(p, axis=1)
    return c - c[:, :1]


def gla_chunk_scan(q, k, v, logf):
    B, T, H, dk = q.shape
    dv = v.shape[-1]
    C = HG_CHUNK
    nc = T // C

    def chunks(x):
        return x.reshape(B, nc, C, H, x.shape[-1]).transpose(1, 0, 3, 2, 4)

    tri = np.tril(np.ones((C, C), dtype=bool))[:, :, None]

    def step(S, inp):
        qc, kc, vc, gc = inp
        b = jnp.cumsum(gc, axis=-2)
        dec = jnp.exp(jnp.where(tri, b[..., :, None, :] - b[..., None, :, :], -jnp.inf))
        attn = jnp.einsum('bhtk,bhsk,bhtsk->bhts', qc, kc, dec)
        o = jnp.einsum('bhts,bhsv->bhtv', attn, vc) + jnp.einsum('bhtk,bhkv->bhtv', qc * jnp.exp(b), S)
        b_last = b[..., -1:, :]
        S = jnp.exp(b_last[..., 0, :])[..., None] * S + jnp.einsum('bhsk,bhsv->bhkv', kc * jnp.exp(b_last - b), vc)
        return S, o

    S0 = jnp.zeros((B, H, dk, dv), jnp.float32)
    _, o = lax.scan(step, S0, (chunks(q), chunks(k), chunks(v), chunks(logf)))
    return o.transpose(1, 0, 3, 2, 4).reshape(B, T, H, dv)


def hgrn2_mixer(cq, cff, cfb, ci, cg, lb_f, lb_b, norm_g):
    B, T, _ = cq.shape
    q = jax.nn.silu(cq.astype(jnp.float32)).reshape(B, T, HG_HEADS, HG_DK)
    v = ci.astype(jnp.float32).reshape(B, T, HG_HEADS, HG_DV)

    def gates(z, lb):
        z = z.astype(jnp.float32).reshape(B, T, HG_HEADS, HG_DK)
        lb = lb.reshape(HG_HEADS, HG_DK)
        logf = jnp.logaddexp(jnp.log(lb), jnp.log1p(-lb) + jax.nn.log_sigmoid(z))
        kk = (1.0 - lb) * jax.nn.sigmoid(-z)
        return logf, kk

    lf_f, k_f = gates(cff, lb_f)
    lf_b, k_b = gates(cfb, lb_b)
    o_f = gla_chunk_scan(q, k_f, v, lf_f)
    rev = lambda a: jnp.flip(a, axis=1)
    o_b = rev(gla_chunk_scan(rev(q), rev(k_b), rev(v), rev(lf_b)))
    o = rmsnorm(o_f + o_b, norm_g) * jax.nn.silu(cg.astype(jnp.float32)).reshape(B, T, HG_HEADS, HG_DV)
    return o.reshape(B, T, HG_WIDTH)


def encoder_layer(x, ln_mix_g, w_in, na_q_g, na_k_g, na_rpb, da_q_g, da_k_g, t5_bias,
                  lb_f, lb_b, hg_norm_g, w_out, ln_ffn_g, w_up, conv_w, conv_b, w_down):
    B, T, _ = x.shape
    scale = HEAD_DIM ** -0.5
    h = rmsnorm(x, ln_mix_g)
    proj = h @ w_in
    (aq, ak, av, bq, bk, bv, cq, cff, cfb, ci, cg) = jnp.split(proj, np.cumsum(IN_SPLITS)[:-1].tolist(), axis=-1)
    heads = lambda a, n: a.reshape(B, T, n, HEAD_DIM)
    qa = rmsnorm(heads(aq, NA_HEADS), na_q_g) * scale
    ka = rmsnorm(heads(ak, NA_HEADS), na_k_g)
    o_a = neighbourhood_attention(qa, ka, heads(av, NA_HEADS), na_rpb)
    qb = rmsnorm(heads(bq, DA_HEADS), da_q_g) * scale
    kb = rmsnorm(heads(bk, DA_HEADS), da_k_g)
    o_b = dilated_attention(qb, kb, heads(bv, DA_HEADS), t5_bias)
    o_c = hgrn2_mixer(cq, cff, cfb, ci, cg, lb_f, lb_b, hg_norm_g)
    mix = jnp.concatenate([o_a, o_b, o_c], axis=-1).astype(x.dtype) @ w_out
    x = x + mix.astype(x.dtype)
    h = rmsnorm(x, ln_ffn_g)
    gate, up = jnp.split(h @ w_up, 2, axis=-1)
    gp = jnp.pad(gate, ((0, 0), (1, 1), (0, 0)))
    gate = gp[:, :-2] * conv_w[0] + gp[:, 1:-1] * conv_w[1] + gp[:, 2:] * conv_w[2] + conv_b
    y = (jax.nn.gelu(gate) * up) @ w_down
    return x + y.astype(x.dtype)


def setup_inputs(seed: int = 0) -> dict:
    key = jax.random.key(seed)
    ks = jax.random.split(key, 20)
    f32 = jnp.float32
    nrm = lambda k, shape, s: jax.random.normal(k, shape, f32) * s
    return {
        "x_prompt": nrm(ks[0], (BATCH, SEQ, D_MODEL), 1.0),
        "x_sample": nrm(ks[1], (DEC_BATCH, DEC_SEQ, D_MODEL), 1.0),
        "ln_mix_g": 1.0 + nrm(ks[2], (DEPTH, D_MODEL), 0.1),
        "w_in": nrm(ks[3], (DEPTH, D_MODEL, N_IN), D_MODEL ** -0.5),
        "na_q_g": 1.0 + nrm(ks[4], (DEPTH, HEAD_DIM), 0.1),
        "na_k_g": 1.0 + nrm(ks[5], (DEPTH, HEAD_DIM), 0.1),
        "na_rpb": nrm(ks[6], (DEPTH, NA_HEADS, 2 * NA_WIN_ROWS - 1, 2 * NA_WIN_COLS - 1), 0.5),
        "da_q_g": 1.0 + nrm(ks[7], (DEPTH, HEAD_DIM), 0.1),
        "da_k_g": 1.0 + nrm(ks[8], (DEPTH, HEAD_DIM), 0.1),
        "t5_bias": nrm(ks[9], (N_BUCKETS, DA_HEADS), 0.5),
        "hg_lb_logits": nrm(ks[10], (2, DEPTH, HG_FDIM), 1.0),
        "hg_norm_g": 1.0 + nrm(ks[11], (DEPTH, HG_DV), 0.1),
        "w_out": nrm(ks[12], (DEPTH, MIX_WIDTH, D_MODEL), MIX_WIDTH ** -0.5),
        "ln_ffn_g": 1.0 + nrm(ks[13], (DEPTH, D_MODEL), 0.1),
        "w_up": nrm(ks[14], (DEPTH, D_MODEL, 2 * D_FF), D_MODEL ** -0.5),
        "conv_w": nrm(ks[15], (DEPTH, 3, D_FF), 3 ** -0.5),
        "conv_b": nrm(ks[16], (DEPTH, D_FF), 0.01),
        "w_down": nrm(ks[17], (DEPTH, D_FF, D_MODEL), D_FF ** -0.5),
    }


def reference(x_prompt, x_sample, ln_mix_g, w_in, na_q_g, na_k_g, na_rpb, da_q_g, da_k_g, t5_bias,
              hg_lb_logits, hg_norm_g, w_out, ln_ffn_g, w_up, conv_w, conv_b, w_down):
    lb = hgrn_lower_bounds(hg_lb_logits)

    def trunk(x):
        for l in range(DEPTH):
            x = encoder_layer(x, ln_mix_g[l], w_in[l], na_q_g[l], na_k_g[l], na_rpb[l],
                              da_q_g[l], da_k_g[l], t5_bias, lb[0, l], lb[1, l], hg_norm_g[l],
                              w_out[l], ln_ffn_g[l], w_up[l], conv_w[l], conv_b[l], w_down[l])
        return x

    y_prompt = trunk(x_prompt)
    y_sample = trunk(x_sample)
    return (y_prompt, y_sample)
```

```python
import numpy as np
import concourse.bass as bass
import concourse.mybir as mybir
from concourse.bass_utils import run_bass_kernel_spmd

F32 = mybir.dt.float32
BF16 = mybir.dt.bfloat16
I32 = mybir.dt.int32
AF = mybir.ActivationFunctionType
ALU = mybir.AluOpType
AX = mybir.AxisListType

D_MODEL = 1024
N_IN = 4096
D_FF = 2816
NFG = D_FF // 128
EPS = 1e-6
NEG = -30000.0
PAD = 1024
DBG = {}
GELU_C = 1.5957691216057308


class Buf:
    __slots__ = ("name", "t", "lastw", "reads", "dram", "W", "R", "key", "psum")

    def __init__(self, name, t, dram=False):
        self.name = name
        self.key = name
        self.t = t
        self.lastw = None
        self.reads = []
        self.dram = dram
        self.psum = False
        self.W = {}
        self.R = {}

    def __getitem__(self, idx):
        return self.t[idx]


class Ins:
    __slots__ = ("eng", "fn", "deps", "dma", "key", "sem", "val", "needed", "seq", "phase")


class Prog:
    ENGS = ("pe", "act", "dve", "pool", "sp")

    def __init__(self, nc, arena_bytes=206 * 1024):
        self.nc = nc
        self.q = {e: [] for e in self.ENGS}
        self.n = 0
        self.nbuf = 0
        self.arena_bytes = arena_bytes
        self.arena = nc.alloc_sbuf_tensor("arena", [128, arena_bytes], mybir.dt.uint8)
        self.ptr = 0
        self.peak = 0
        self.last_c = {}
        self.last_d = {}
        self.bar = {e: None for e in self.ENGS}
        self.phase = 0

    def sb(self, name, shape, dtype):
        esz = 2 if dtype == BF16 else 4
        n = 1
        for d in shape[1:]:
            n *= d
        nb = n * esz
        off = self.ptr
        self.ptr += (nb + 31) // 32 * 32
        self.peak = max(self.peak, self.ptr)
        assert self.ptr <= self.arena_bytes, f"SBUF arena overflow allocating {name} {shape}: {self.ptr}"
        ap = self.arena[0:shape[0], off:off + nb].bitcast(dtype)
        if len(shape) > 2:
            names = "abcdef"[:len(shape) - 1]
            pat = "p (" + " ".join(names) + ") -> p " + " ".join(names)
            ap = ap.rearrange(pat, **{nm: d for nm, d in zip(names, shape[1:])})
        return Buf(name, ap)

    def barrier(self):
        deps = list(self.last_c.values()) + list(self.last_d.values())
        self.phase += 1
        for e in self.ENGS:
            self.bar[e] = (self.bar[e] or []) + list(deps)

    def ps(self, name, shape, dtype=F32):
        self.nbuf += 1
        b = Buf(name, self.nc.alloc_psum_tensor(f"{name}_{self.nbuf}", list(shape), dtype))
        b.psum = True
        return b

    def dram(self, name, shape, dtype, kind="Internal", strict=False):
        return Buf(name, self.nc.dram_tensor(name, list(shape), dtype, kind=kind), dram=not strict)

    def op(self, eng, fn, reads=(), writes=(), dma=False, key=None):
        I = Ins()
        I.eng, I.fn, I.dma, I.key = eng, fn, dma, key
        I.sem, I.val, I.needed, I.seq = None, 0, False, self.n
        I.phase = self.phase
        self.n += 1
        sk = ("dma", key) if dma else ("eng", eng)
        pr = [b for b in reads if b.psum]
        if pr:
            reads = [b for b in reads if not b.psum]
            writes = list(writes) + [b for b in pr if b not in writes]
        deps = {}
        for b in reads:
            if b.dram:
                for d in b.W.values():
                    deps[id(d)] = d
            elif b.lastw is not None:
                deps[id(b.lastw)] = b.lastw
        for b in writes:
            if b.dram:
                for d in b.R.values():
                    deps[id(d)] = d
            else:
                if b.lastw is not None:
                    deps[id(b.lastw)] = b.lastw
                for r in b.reads:
                    deps[id(r)] = r
        for b in reads:
            if b.dram:
                b.R[sk] = I
            else:
                b.reads.append(I)
        for b in writes:
            if b.dram:
                b.W[sk] = I
            else:
                b.lastw = I
                b.reads = []
        if self.bar[eng] is not None:
            for d in self.bar[eng]:
                deps[id(d)] = d
            self.bar[eng] = None
        if dma:
            self.last_d[key] = I
        else:
            self.last_c[eng] = I
        deps.pop(id(I), None)
        dl = []
        for d in deps.values():
            if d.eng == "pe" and eng == "pe" and not d.dma and not dma:
                continue
            d.needed = True
            dl.append(d)
        I.deps = dl
        self.q[eng].append(I)
        return I

    def emit(self):
        nc = self.nc
        sems = {e: nc.alloc_semaphore(f"s_{e}") for e in self.ENGS}
        cnt = {e: 0 for e in self.ENGS}
        kcnt, ksem = {}, {}
        pidx = {}
        allins = sorted((i for e in self.ENGS for i in self.q[e]), key=lambda i: i.seq)
        for I in allins:
            if I.dma:
                pk = (I.phase, I.key)
                if pk not in pidx:
                    pidx[pk] = sum(1 for q in pidx if q[0] == I.phase)
                k = pidx[pk]
                if k not in ksem:
                    ksem[k] = nc.alloc_semaphore(f"d_{len(ksem)}")
                    kcnt[k] = 0
                kcnt[k] += 16
                I.sem, I.val = ksem[k], kcnt[k]
            elif I.needed:
                cnt[I.eng] += 1
                I.sem, I.val = sems[I.eng], cnt[I.eng]
        self.nsem = len(ksem) + len(sems)
        handles = {"pe": "tensor", "act": "scalar", "dve": "vector", "pool": "gpsimd", "sp": "sync"}
        final_dma = [(ksem[k], kcnt[k]) for k in ksem]

        def body(ename):
            def f(e):
                known = {}
                for I in self.q[ename]:
                    need = {}
                    for d in I.deps:
                        sid = id(d.sem)
                        if sid not in need or need[sid][1] < d.val:
                            need[sid] = (d.sem, d.val)
                    for sid, (sem, val) in need.items():
                        if known.get(sid, 0) < val:
                            e.wait_ge(sem, val)
                            known[sid] = val
                    r = I.fn(e)
                    if I.dma:
                        r.then_inc(I.sem, 16)
                    elif I.needed:
                        r.then_inc(I.sem, 1)
                if ename == "sp":
                    for s, v in final_dma:
                        e.wait_ge(s, v)
            return f

        with nc.Block() as block:
            for ename in self.ENGS:
                getattr(block, handles[ename])(body(ename))


class Rot:
    def __init__(self, bufs):
        self.bufs = bufs
        self.i = 0
        for j, b in enumerate(bufs):
            b.key = f"{b.name}{j}"

    def next(self):
        b = self.bufs[self.i % len(self.bufs)]
        self.i += 1
        return b


def t5_bucket_np(rel):
    nb = 16
    max_exact = 8
    ret = np.where(rel > 0, nb, 0)
    n = np.abs(rel)
    large = max_exact + (np.log(np.maximum(n, 1) / max_exact) / np.log(1024 / max_exact) * (nb - max_exact)).astype(np.int32)
    large = np.minimum(large, nb - 1)
    return (ret + np.where(n < max_exact, n, large)).astype(np.int32)


def host_consts():
    c = {}
    qc = 63 - np.arange(64)[:, None]
    kc = np.arange(64)[None, :]
    c0 = np.clip(qc - 8, 0, 48)
    ok = (kc >= c0) & (kc < c0 + 16)
    c["c_maskA"] = np.where(ok, 0.0, NEG).astype(np.float32)
    Gf = np.zeros((33, 3, 384), np.float32)
    for di, d in enumerate((1, 4, 16)):
        for j in range(383):
            delta = j - 191
            if abs(delta) <= 64:
                Gf[t5_bucket_np(np.array(delta * d)), di, j] = 1.0
            else:
                Gf[32, di, j] = 1.0
        Gf[32, di, 383] = 1.0
    c["c_Gf"] = Gf
    s = np.arange(64)[:, None, None]
    t = np.arange(64)[None, None, :]
    c["c_mf"] = np.broadcast_to((s <= t), (64, 4, 64)).astype(np.int32).copy()
    c["c_mb"] = np.broadcast_to((s >= t), (64, 4, 64)).astype(np.int32).copy()
    mf = np.ones((128, 8, 64), np.float32)
    mf[:, :, 0] = 0.0
    mb = np.ones((128, 8, 64), np.float32)
    mb[:, :, 63] = 0.0
    c["c_smf"] = mf.reshape(128, 512)
    c["c_smb"] = mb.reshape(128, 512)
    ident = np.eye(128, dtype=np.float32)
    c["c_ident"] = ident
    c["c_anti"] = ident[::-1].copy()
    a64 = np.zeros((128, 64), np.float32)
    a64[:64] = np.eye(64, dtype=np.float32)[::-1]
    c["c_anti64"] = a64
    return c


CONST_SHAPES = {"c_maskA": ([64, 64], F32), "c_Gf": ([33, 3, 384], F32), "c_mf": ([64, 4, 64], I32), "c_mb": ([64, 4, 64], I32),
                "c_smf": ([128, 512], F32), "c_smb": ([128, 512], F32), "c_ident": ([128, 128], F32), "c_anti": ([128, 128], F32),
                "c_anti64": ([128, 64], F32)}

W_SHAPES = {"ln_mix_g": [2, 1024], "w_in": [2, 1024, 4096], "na_q_g": [2, 64], "na_k_g": [2, 64], "na_rpb": [2, 4, 15, 31],
            "da_q_g": [2, 64], "da_k_g": [2, 64], "t5_bias": [32, 4], "hg_lb_logits": [2, 2, 512], "hg_norm_g": [2, 128],
            "w_out": [2, 1024, 1024], "ln_ffn_g": [2, 1024], "w_up": [2, 1024, 5632], "conv_w": [2, 3, 2816], "conv_b": [2, 2816],
            "w_down": [2, 2816, 1024]}


def build(seq_lens, depth=2, debug=(), stop_after=None):
    nc = bass.Bass("TRN2", target_bir_lowering=False)
    P = Prog(nc)
    op = P.op
    NS = len(seq_lens)
    Tall = sum(seq_lens)
    offs, xoffs = [], []
    o, xo = PAD, 0
    for T in seq_lens:
        offs.append(o)
        xoffs.append(xo)
        o += T + PAD
        xo += T
    TP = o

    def dkind(name):
        return "ExternalOutput" if name in debug else "Internal"

    x_in = P.dram("x", [Tall, D_MODEL], F32, kind="ExternalInput")
    y_out = P.dram("y", [Tall, D_MODEL], F32, kind="ExternalOutput")
    Wd = {k: P.dram(k, s, F32, kind="ExternalInput") for k, s in W_SHAPES.items()}
    Cd = {k: P.dram(k, s, dt, kind="ExternalInput") for k, (s, dt) in CONST_SHAPES.items()}

    def scratch(name, shape, dt, strict=False):
        return P.dram(name, shape, dt, kind=dkind(name), strict=strict)

    X1 = scratch("X1", [Tall, D_MODEL], F32)
    YP = scratch("YP", [Tall, D_MODEL], F32)
    XL = scratch("XL", [Tall, D_MODEL], F32)
    QKTA = scratch("QKTA", [512, TP], BF16)
    VA = scratch("VA", [TP, 260], BF16)
    QB = scratch("QB", [TP, 256], BF16)
    KB = scratch("KB", [TP, 256], BF16)
    VB = scratch("VB", [TP, 260], BF16)
    HQ = scratch("HQ", [2, 512, TP], BF16)
    HK = scratch("HK", [2, 512, TP], BF16)
    HV = scratch("HV", [TP, 512], BF16)
    HG = scratch("HG", [TP, 512], F32)
    HEM = scratch("HEM", [2, 512, TP // 64], F32)
    HEL = scratch("HEL", [2, 512, TP // 64], F32)
    OFB = [scratch("OF", [TP, 512], F32), scratch("OB", [TP, 512], F32)]
    ACCB = [scratch(f"ACCB{i}", [TP, 260], F32) for i in range(3)]
    MIX = scratch("MIX", [TP, 1024], BF16)
    H2T = scratch("H2T", [1024, TP], BF16)
    FB = scratch("FB", [3, 4, 384], BF16, strict=True)
    RP = scratch("RP", [2, 4, 15, 160], F32, strict=True)

    RW = P.sb("RW", [128, 33792], BF16)

    def Win(c, lo, hi):
        return RW[:, c * 4096 + lo:c * 4096 + hi]

    def Wout(c, lo, hi):
        return RW[:, c * 1024 + lo:c * 1024 + hi]

    def WupG(c, j):
        return RW[:, c * 2816 + j * 128:c * 2816 + (j + 1) * 128]

    def WupU(c, j):
        return RW[:, c * 2816 + 1408 + j * 128:c * 2816 + 1408 + (j + 1) * 128]

    def Wdn(j, lo, hi):
        return RW[:, 22528 + j * 1024 + lo:22528 + j * 1024 + hi]

    identb = P.sb("identb", [128, 128], BF16)
    antib = P.sb("antib", [128, 128], BF16)
    anti64b = P.sb("anti64b", [128, 64], BF16)
    ctmp = P.sb("ctmp", [128, 128], F32)
    lng = P.sb("lng", [128, 2, 2, 8], F32)
    GQK = P.sb("GQK", [128, 2, 2, 2, 64], F32)
    gtmp = P.sb("gtmp", [128, 64], F32)
    LBL = P.sb("LBL", [128, 2, 2, 4], F32)
    LB = P.sb("LB", [128, 2, 4], F32)
    LN1 = P.sb("LN1", [128, 2, 4], F32)
    HGG = P.sb("HGG", [128, 2, 128], F32)
    CVW = P.sb("CVW", [128, 2, 4, NFG], F32)
    R2A = P.sb("R2A", [128, 60 * 64], BF16)
    maskA = P.sb("maskA", [64, 64], F32)
    TBs = P.sb("TBs", [128, 3, 4, 2, 128], BF16)
    gf = P.sb("gf", [33, 384], F32)
    t5e = P.sb("t5e", [33, 4], F32)
    fbs = P.sb("fbs", [4, 384], BF16)
    mfm = P.sb("mfm", [64, 256], I32)
    mbm = P.sb("mbm", [64, 256], I32)
    smf = P.sb("smf", [128, 512], F32)
    smb = P.sb("smb", [128, 512], F32)
    zt = P.sb("zt", [128, 260], BF16)
    zf = P.sb("zf", [128, 160], F32)

    PS = [P.ps("ps", [128, 512], F32) for _ in range(8)]
    psr = Rot(PS[2:8])
    pT, pT2 = PS[0], PS[1]

    def bfv(ps):
        return ps[:].bitcast(BF16)

    def dma(eng, out, in_, reads, writes, key, slow=False):
        if slow:
            return op(eng, lambda e: e.dma_start(out=out, in_=in_, allow_slow_non_contiguous=True), reads=reads, writes=writes, dma=True, key=key)
        return op(eng, lambda e: e.dma_start(out=out, in_=in_), reads=reads, writes=writes, dma=True, key=key)

    def act(out, in_, func, reads, writes, **kw):
        return op("act", lambda e: e.activation(out=out, in_=in_, func=func, **kw), reads=reads, writes=writes)

    def tt(eng, out, in0, in1, alu, reads, writes):
        return op(eng, lambda e: e.tensor_tensor(out=out, in0=in0, in1=in1, op=alu), reads=reads, writes=writes)

    def ts(eng, out, in0, s1, s2, op0, op1, reads, writes):
        if op1 is None:
            return op(eng, lambda e: e.tensor_scalar(out=out, in0=in0, scalar1=s1, scalar2=None, op0=op0), reads=reads, writes=writes)
        return op(eng, lambda e: e.tensor_scalar(out=out, in0=in0, scalar1=s1, scalar2=s2, op0=op0, op1=op1), reads=reads, writes=writes)

    def stt(out, in0, scalar, in1, op0, op1, reads, writes):
        return op("dve", lambda e: e.scalar_tensor_tensor(out=out, in0=in0, scalar=scalar, in1=in1, op0=op0, op1=op1), reads=reads, writes=writes)

    def cp(eng, out, in_, reads, writes):
        if eng == "act":
            return op(eng, lambda e: e.activation(out=out, in_=in_, func=AF.Copy), reads=reads, writes=writes)
        return op(eng, lambda e: e.tensor_copy(out=out, in_=in_), reads=reads, writes=writes)

    def mm(out, lhsT, rhs, start, stop, reads, writes):
        return op("pe", lambda e: e.matmul(out, lhsT=lhsT, rhs=rhs, start=start, stop=stop), reads=reads, writes=writes)

    def tr(out, in_, ident, reads, writes):
        return op("pe", lambda e: e.transpose(out=out, in_=in_, identity=ident), reads=reads, writes=writes)

    op("pool", lambda e: e.memset(zt[:], 0.0), writes=[zt])
    op("pool", lambda e: e.memset(R2A[:], 0.0), writes=[R2A])
    op("pool", lambda e: e.memset(zf[:], 0.0), writes=[zf])
    for src, dst in ((Cd["c_ident"], identb), (Cd["c_anti"], antib)):
        dma("sp", ctmp[:], src[:], [src], [ctmp], "ctmp")
        cp("dve", dst[:], ctmp[:], [ctmp], [dst])
    dma("sp", ctmp[:, 0:64], Cd["c_anti64"][:], [Cd["c_anti64"]], [ctmp], "ctmp")
    cp("dve", anti64b[:], ctmp[:, 0:64], [ctmp], [anti64b])
    dma("sp", mfm[:], Cd["c_mf"][:].rearrange("s h t -> s (h t)"), [], [mfm], "mfm")
    dma("sp", mbm[:], Cd["c_mb"][:].rearrange("s h t -> s (h t)"), [], [mbm], "mbm")
    dma("sp", smf[:], Cd["c_smf"][:], [], [smf], "smf")
    dma("sp", smb[:], Cd["c_smb"][:], [], [smb], "smb")
    dma("sp", maskA[:], Cd["c_maskA"][:], [], [maskA], "maskA")
    for l in range(depth):
        dma("sp", lng[:, l, 0, :], Wd["ln_mix_g"][l, :].rearrange("(c p) -> p c", p=128), [], [lng], "lng", slow=True)
        dma("sp", lng[:, l, 1, :], Wd["ln_ffn_g"][l, :].rearrange("(c p) -> p c", p=128), [], [lng], "lng", slow=True)
    for l in range(depth):
        for ab, (qn, kn) in enumerate((("na_q_g", "na_k_g"), ("da_q_g", "da_k_g"))):
            for qk, nm in enumerate((qn, kn)):
                dma("sp", gtmp[:], Wd[nm][l, :].partition_broadcast(128), [], [gtmp], "gtmp")
                dstv = GQK[:, l, ab, qk, :]
                srcv = gtmp[:]
                if qk == 0:
                    ts("dve", dstv, srcv, 0.125, None, ALU.mult, None, [gtmp], [GQK])
                else:
                    cp("dve", dstv, srcv, [gtmp], [GQK])
    dma("sp", LBL[:], Wd["hg_lb_logits"][:].rearrange("d l (h k) -> k d l h", k=128), [], [LBL], "LBL", slow=True)
    tt("dve", LB[:], LBL[:, :, 1, :], LBL[:, :, 0, :], ALU.subtract, [LBL], [LB])
    act(LN1[:], LB[:], AF.Exp, [LB], [LN1])
    act(LN1[:], LN1[:], AF.Ln, [LN1], [LN1], bias=1.0)
    tt("dve", LB[:], LB[:], LN1[:], ALU.subtract, [LB, LN1], [LB])
    act(LB[:], LB[:], AF.Exp, [LB], [LB])
    ts("dve", LN1[:], LN1[:], -1.0, None, ALU.mult, None, [LN1], [LN1])
    for l in range(depth):
        dma("sp", HGG[:, l, :], Wd["hg_norm_g"][l, :].partition_broadcast(128), [], [HGG], "HGG")
        for j in range(3):
            dma("sp", CVW[:, l, j, :], Wd["conv_w"][l, j, :].rearrange("(g p) -> p g", p=128), [], [CVW], "CVW", slow=True)
        dma("sp", CVW[:, l, 3, :], Wd["conv_b"][l, :].rearrange("(g p) -> p g", p=128), [], [CVW], "CVW", slow=True)

    pad_starts = [0] + [offs[s] + seq_lens[s] for s in range(NS)]
    for ps_ in pad_starts:
        for cc in range(PAD // 128):
            dma("sp", KB[ps_ + cc * 128:ps_ + (cc + 1) * 128, :], zt[:, 0:256], [zt], [KB], "zt")
            dma("sp", VB[ps_ + cc * 128:ps_ + (cc + 1) * 128, :], zt[:, 0:260], [zt], [VB], "zt")
        for col in (ps_ + PAD - 1, ps_):
            dma("sp", H2T[:, col:col + 1].rearrange("(g p) t -> p g t", p=128), zt[:, 0:8].rearrange("p (g t) -> p g t", t=1), [zt], [H2T], "zt", slow=True)
    dma("sp", RP[:].rearrange("l h r d -> (l h r) d"), zf[0:120, 0:160], [zf], [RP], "zf")
    for l in range(depth):
        dma("sp", RP[l, :, :, 64:95], Wd["na_rpb"][l], [Wd["na_rpb"]], [RP], "rpcopy", slow=True)
    op("pool", lambda e: e.memset(t5e[:], NEG), writes=[t5e])
    dma("sp", t5e[0:32, :], Wd["t5_bias"][:], [], [t5e], "t5e")
    for di in range(3):
        ps = psr.next()
        dma("sp", gf[:], Cd["c_Gf"][:, di, :], [], [gf], "gf")
        mm(ps[0:4, 0:384], t5e[:], gf[:], True, True, [t5e, gf], [ps])
        cp("dve", fbs[:], ps[0:4, 0:384], [ps], [fbs])
        dma("sp", FB[di], fbs[:], [fbs], [FB], "fbs")
    for di in range(3):
        for h in range(4):
            src = bass.AP(FB.t, (di * 4 + h) * 384, [[1, 128], [128, 2], [1, 128]])
            dma("sp", TBs[:, di, h, :, :], src, [FB], [TBs], "TBs", slow=True)

    def load_weight(jobs, wst):
        for k, (src, dst, g) in enumerate(jobs):
            st = wst.next()
            n = src.shape[1]
            dma("sp", st[:, 0:n], src, [], [st], st.key)
            eng = "dve" if k % 2 == 0 else "pool"
            if g is not None:
                ts(eng, dst, st[:, 0:n], g, None, ALU.mult, None, [st, lng], [RW])
            else:
                cp(eng, dst, st[:, 0:n], [st], [RW])

    WIN_PIECES = [(0, 512, 0), (512, 768, 1024), (768, 1280, 512), (1280, 1536, 1280), (1536, 2048, 1536), (2048, 4096, 2048)]

    base_ptr = P.ptr

    def begin_phase():
        P.ptr = base_ptr
        P.barrier()

    class Pools:
        pass

    pl = Pools()

    def common_pools(nx=2):
        pl.xs = Rot([P.sb("xs", [128, 1024], F32) for _ in range(nx)])
        pl.xn = Rot([P.sb("xn", [128, 1024], BF16) for _ in range(2)])
        pl.junk = P.sb("junk", [128, 1024], BF16)
        pl.st1 = Rot([P.sb("st1", [128, 8], F32) for _ in range(6)])
        pl.hT = Rot([P.sb("hT", [128, 8, 130], BF16) for _ in range(2)])

    def rmsnorm_to_hT(X, gsel_unused, HT, ncols=128, col0=0):
        S = pl.st1.next()
        XN = pl.xn.next()
        junk = pl.junk
        act(junk[:], X[:], AF.Square, [X], [junk, S], accum_out=S[:, 0:1])
        act(S[:, 1:2], S[:, 0:1], AF.Ln, [S], [S], scale=1.0 / D_MODEL, bias=EPS)
        act(S[:, 2:3], S[:, 1:2], AF.Exp, [S], [S], scale=-0.5)
        act(XN[:], X[:], AF.Copy, [X, S], [XN], scale=S[:, 2:3])
        for c in range(8):
            tr(bfv(pT)[:, c * 128:(c + 1) * 128], XN[:, c * 128:(c + 1) * 128], identb[:], [XN, identb], [pT])
        cp("dve", HT[:, :, col0:col0 + 128], bfv(pT).rearrange("p (c t) -> p c t", t=128), [pT], [HT])

    def phase_A(l):
        begin_phase()
        common_pools()
        xs, st1 = pl.xs, pl.st1
        hT5 = Rot([P.sb("hT5", [128, 8, 512], BF16) for _ in range(2)])
        wst = Rot([P.sb("wst", [128, 2048], F32) for _ in range(2)])
        f512 = Rot([P.sb("f512", [128, 512], F32) for _ in range(10)])
        b512 = Rot([P.sb("b512", [128, 512], BF16) for _ in range(6)])
        vv = Rot([P.sb("vv", [128, 8, 65], BF16) for _ in range(2)])
        for v_ in vv.bufs:
            op("pool", lambda e, v_=v_: e.memset(v_[:], 1.0), writes=[v_])
        qkT = Rot([P.sb("qkT", [128, 4, 128], BF16) for _ in range(2)])
        CQ = P.sb("CQ", [128, 4, 512], F32)
        LQ = P.sb("LQ", [128, 4, 512], F32)
        jobs = []
        for c in range(8):
            for (a, b, d0) in WIN_PIECES:
                jobs.append((Wd["w_in"][l, c * 128:(c + 1) * 128, a:b], Win(c, d0, d0 + b - a), lng[:, l, 0, c:c + 1]))
        load_weight(jobs, wst)
        xsrc = x_in if l == 0 else XL
        for s in range(NS):
            for i5 in range(seq_lens[s] // 512):
                HT = hT5.next()
                for sub in range(4):
                    i = i5 * 4 + sub
                    xg = xoffs[s] + i * 128
                    tp = offs[s] + i * 128
                    X = xs.next()
                    dma("sp", X[:], xsrc[xg:xg + 128, :], [xsrc], [X], X.key)
                    rmsnorm_to_hT(X, None, HT, col0=sub * 128)

                    def tok_group(lo, hi):
                        ps = psr.next()
                        for c in range(8):
                            mm(ps[:, 0:hi - lo], HT[:, c, sub * 128:(sub + 1) * 128], Win(c, lo, hi), c == 0, c == 7, [HT, RW], [ps])
                        return ps

                    for ab in range(2):
                        ps = tok_group(ab * 512, ab * 512 + 512)
                        SQ = f512.next()
                        S = st1.next()
                        act(SQ[:], ps[:], AF.Square, [ps], [SQ])
                        op("dve", lambda e, S=S, SQ=SQ: e.tensor_reduce(out=S[:, 0:8], in_=SQ[:].rearrange("p (h d) -> p h d", d=64), axis=AX.X, op=ALU.add),
                           reads=[SQ], writes=[S])
                        act(S[:, 0:8], S[:, 0:8], AF.Ln, [S], [S], scale=1.0 / 64, bias=EPS)
                        act(S[:, 0:8], S[:, 0:8], AF.Exp, [S], [S], scale=-0.5)
                        T1 = f512.next()
                        tt("dve", T1[:].rearrange("p (h d) -> p h d", d=64), ps[:].rearrange("p (h d) -> p h d", d=64),
                           S[:, 0:8].unsqueeze(2).broadcast_to([128, 8, 64]), ALU.mult, [ps, S], [T1])
                        O = b512.next()
                        tt("pool", O[:].rearrange("p (a h d) -> p a h d", a=2, h=4), T1[:].rearrange("p (a h d) -> p a h d", a=2, h=4),
                           GQK[:, l, ab, :, :].unsqueeze(2).broadcast_to([128, 2, 4, 64]), ALU.mult, [T1, GQK], [O])
                        if ab == 0:
                            for j in range(4):
                                tr(bfv(pT2)[:, j * 128:(j + 1) * 128], O[:, j * 128:(j + 1) * 128], identb[:], [O, identb], [pT2])
                            QT = qkT.next()
                            cp("act", QT[:], bfv(pT2)[:, 0:512].rearrange("p (j t) -> p j t", t=128), [pT2], [QT])
                            dma("sp", QKTA[:, tp:tp + 128].rearrange("(j p) t -> p j t", p=128), QT[:], [QT], [QKTA], QT.key)
                        else:
                            dma("sp", QB[tp:tp + 128, :], O[:, 0:256], [O], [QB], O.key)
                            dma("sp", KB[tp:tp + 128, :], O[:, 256:512], [O], [KB], O.key)
                    ps = tok_group(1024, 1536)
                    VV = vv.next()
                    cp("act", VV[:, :, 0:64], ps[:].rearrange("p (h d) -> p h d", d=64), [ps], [VV])
                    dma("sp", VA[tp:tp + 128, :], VV[:, 0:4, :].rearrange("p h d -> p (h d)"), [VV], [VA], VV.key)
                    dma("sp", VB[tp:tp + 128, :], VV[:, 4:8, :].rearrange("p h d -> p (h d)"), [VV], [VB], VV.key)
                    ps = tok_group(3072, 3584)
                    O = b512.next()
                    cp("act", O[:], ps[:], [ps], [O])
                    dma("sp", HV[tp:tp + 128, :], O[:], [O], [HV], O.key)
                    ps = tok_group(3584, 4096)
                    E = f512.next()
                    act(E[:], ps[:], AF.Exp, [ps], [E], scale=-1.0)
                    act(E[:], E[:], AF.Ln, [E], [E], bias=1.0)
                    act(E[:], E[:], AF.Exp, [E], [E], scale=-1.0)
                    G = f512.next()
                    tt("dve", G[:], ps[:], E[:], ALU.mult, [ps, E], [G])
                    dma("sp", HG[tp:tp + 128, :], G[:], [G], [HG], G.key)

                tp5 = offs[s] + i5 * 512

                def feat_head(lo, hh):
                    ps = psr.next()
                    for c in range(8):
                        mm(ps[:], Win(c, lo + hh * 128, lo + (hh + 1) * 128), HT[:, c, :], c == 0, c == 7, [HT, RW], [ps])
                    return ps

                for hh in range(4 if DBG.get("PA", 9) >= 2 else 0):
                    ps = feat_head(1536, hh)
                    if DBG.get("PB", 9) >= 1:
                        cp("dve", CQ[:, hh, :], ps[:], [ps], [CQ])
                    if DBG.get("PB", 9) >= 2:
                        act(LQ[:, hh, :], ps[:], AF.Exp, [ps], [LQ], scale=-1.0)
                    if DBG.get("PB", 9) >= 3:
                        act(LQ[:, hh, :], LQ[:, hh, :], AF.Ln, [LQ], [LQ], bias=1.0)
                for dr in range(2 if DBG.get("PA", 9) >= 3 else 0):
                    for hh in range(4):
                        ps = feat_head(2048 + dr * 512, hh)
                        E = f512.next()
                        L1 = f512.next()
                        act(E[:], ps[:], AF.Exp, [ps], [E], scale=-1.0)
                        act(L1[:], E[:], AF.Ln, [E], [L1], bias=1.0)
                        LF = f512.next()
                        if l == 0:
                            ts("pool", LF[:], L1[:], -1.0, None, ALU.mult, None, [L1], [LF])
                        else:
                            act(LF[:], E[:], AF.Ln, [E, LB], [LF], scale=LB[:, dr, hh:hh + 1], bias=1.0)
                            tt("pool", LF[:], LF[:], L1[:], ALU.subtract, [LF, L1], [LF])
                        Bc = f512.next()
                        if dr == 0:
                            op("dve", lambda e, Bc=Bc, LF=LF: e.tensor_tensor_scan(out=Bc[:], data0=smf[:], data1=LF[:], initial=0.0, op0=ALU.mult, op1=ALU.add),
                               reads=[smf, LF], writes=[Bc])
                            mi, li = 31, 63
                        else:
                            op("dve", lambda e, Bc=Bc, LF=LF: e.tensor_tensor_scan(out=Bc[:, ::-1], data0=smb[:, ::-1], data1=LF[:, ::-1], initial=0.0, op0=ALU.mult, op1=ALU.add),
                               reads=[smb, LF], writes=[Bc])
                            mi, li = 32, 0
                        B3 = Bc[:].rearrange("p (c t) -> p c t", t=64)
                        S = st1.next()
                        S2 = st1.next()
                        act(S[:, 0:8], B3[:, :, mi], AF.Exp, [Bc], [S])
                        tt("dve", S2[:, 0:8], B3[:, :, li], B3[:, :, mi], ALU.subtract, [Bc], [S2])
                        act(S2[:, 0:8], S2[:, 0:8], AF.Exp, [S2], [S2])
                        cpi = tp5 // 64
                        dma("sp", HEM[dr, hh * 128:(hh + 1) * 128, cpi:cpi + 8], S[:, 0:8], [S], [HEM], S.key, slow=True)
                        dma("sp", HEL[dr, hh * 128:(hh + 1) * 128, cpi:cpi + 8], S2[:, 0:8], [S2], [HEL], S2.key, slow=True)
                        Dd = f512.next()
                        tt("dve", Dd[:].rearrange("p (c t) -> p c t", t=64), B3, B3[:, :, mi:mi + 1].broadcast_to([128, 8, 64]), ALU.subtract, [Bc], [Dd])
                        ZL = L1
                        tt("dve", ZL[:], ps[:], L1[:], ALU.add, [ps, L1], [ZL])
                        tt("pool", ZL[:], ZL[:], Dd[:], ALU.add, [ZL, Dd], [ZL])
                        KT = b512.next()
                        if l == 0:
                            act(KT[:], ZL[:], AF.Exp, [ZL], [KT], scale=-1.0)
                        else:
                            act(KT[:], ZL[:], AF.Exp, [ZL, LN1], [KT], scale=-1.0, bias=LN1[:, dr, hh:hh + 1])
                        dma("sp", HK[dr, hh * 128:(hh + 1) * 128, tp5:tp5 + 512], KT[:], [KT], [HK], KT.key)
                        tt("pool", Dd[:], Dd[:], LQ[:, hh, :], ALU.subtract, [Dd, LQ], [Dd])
                        act(Dd[:], Dd[:], AF.Exp, [Dd], [Dd])
                        QTt = b512.next()
                        tt("dve", QTt[:], Dd[:], CQ[:, hh, :], ALU.mult, [Dd, CQ], [QTt])
                        dma("sp", HQ[dr, hh * 128:(hh + 1) * 128, tp5:tp5 + 512], QTt[:], [QTt], [HQ], QTt.key)

    def mixer_A(l):
        begin_phase()
        aQ = Rot([P.sb("aQ", [128, 2, 2, 512], BF16) for _ in range(2)])
        for q_ in aQ.bufs:
            op("pool", lambda e, q_=q_: e.memset(q_[:], 0.0), writes=[q_])
        aK = Rot([P.sb("aK", [128, 2, 960], BF16) for _ in range(2)])
        aV = Rot([P.sb("aV", [128, 2, 8, 260], BF16) for _ in range(2)])
        aP = Rot([P.sb("aP", [128, 512], BF16) for _ in range(3)])
        aO = Rot([P.sb("aO", [64, 8, 256], BF16) for _ in range(2)])
        aR = Rot([P.sb("aR", [64, 4], F32) for _ in range(3)])
        r2f = P.sb("r2f", [64, 15 * 64], F32)
        for h in range(4):
            src = bass.AP(RP.t, (l * 4 + h) * 15 * 160 + 16, [[1, 64], [160, 15], [1, 64]])
            dma("sp", r2f[:].rearrange("p (a k) -> p a k", k=64), src, [RP], [r2f], "r2f", slow=True)
            tt("dve", R2A[0:64, h * 960:(h + 1) * 960].rearrange("p (a k) -> p a k", k=64), r2f[:].rearrange("p (a k) -> p a k", k=64),
               maskA[:].unsqueeze(1).broadcast_to([64, 15, 64]), ALU.add, [r2f, maskA], [R2A])
        for s in range(NS):
            T = seq_lens[s]
            rows = T // 64
            base = offs[s]
            for rb in range(0, rows, 8):
                kb = max(0, rb - 4)
                ke = min(rows, rb + 11)
                nk = (ke - kb) * 64
                Q, K, V, O = aQ.next(), aK.next(), aV.next(), aO.next()
                qsrc = QKTA[0:256, base + rb * 64:base + rb * 64 + 512].rearrange("(j p) t -> p j t", p=128)
                dma("sp", Q[0:64, :, 0, :], qsrc[0:64], [QKTA], [Q], (Q.key, 0))
                dma("sp", Q[64:128, :, 1, :], qsrc[64:128], [QKTA], [Q], (Q.key, 1))
                dma("sp", K[:, :, 0:nk], QKTA[256:512, base + kb * 64:base + kb * 64 + nk].rearrange("(j p) t -> p j t", p=128), [QKTA], [K], K.key)
                for par in range(2):
                    dma("sp", V[:, par, :, :], VA[base + (kb + par) * 64:base + (kb + par) * 64 + 1024, :].rearrange("(c p) f -> p c f", p=128), [VA], [V], (V.key, par))
                for ri in range(8 if DBG.get("A", 9) >= 2 else 0):
                    r = rb + ri
                    r0 = min(max(r - 4, 0), rows - 8)
                    delta = r0 - r
                    par = (r0 - kb) % 2
                    vc0 = (r0 - kb - par) // 2
                    pO = psr.next()
                    for hp in range(2):
                        pS = psr.next()
                        for hh in range(2):
                            h = hp * 2 + hh
                            for m in range(4):
                                kc0 = (r0 + 2 * m - kb) * 64
                                dr0 = delta + 2 * m + 7
                                outp = pS[:, (hh * 4 + m) * 64:(hh * 4 + m + 1) * 64]
                                mm(outp, K[:, hp, kc0:kc0 + 128], Q[:, hp, hh, ri * 64:(ri + 1) * 64], True, False, [K, Q], [pS])
                                mm(outp, R2A[:, (h * 15 + dr0) * 64:(h * 15 + dr0 + 2) * 64], anti64b[:, :], False, True, [R2A, anti64b], [pS])
                        Pt = aP.next()
                        act(Pt[:], pS[:], AF.Exp, [pS], [Pt])
                        for hh in range(2 if DBG.get("A", 9) >= 3 else 0):
                            h = hp * 2 + hh
                            for m in range(4):
                                mm(pO[0:64, h * 65:(h + 1) * 65], Pt[:, (hh * 4 + m) * 64:(hh * 4 + m + 1) * 64], V[:, par, vc0 + m, h * 65:(h + 1) * 65], m == 0, m == 3, [Pt, V], [pO])
                    if DBG.get("A", 9) < 4:
                        continue
                    R = aR.next()
                    pO3 = pO[0:64, 0:260].rearrange("p (h d) -> p h d", d=65)
                    op("dve", lambda e, R=R, pO3=pO3: e.reciprocal(out=R[:], in_=pO3[:, :, 64]), reads=[pO], writes=[R])
                    tt("dve", O[:, ri, :].rearrange("p (h d) -> p h d", d=64), pO3[:, :, 0:64], R[:].unsqueeze(2).broadcast_to([64, 4, 64]), ALU.mult, [pO, R], [O])
                dma("sp", MIX[base + rb * 64:base + rb * 64 + 512, 0:256].rearrange("(i p) f -> p i f", p=64), O[:], [O], [MIX], O.key)

    def mixer_B(l):
        begin_phase()
        bQ = Rot([P.sb("bQ", [128, 256], BF16) for _ in range(3)])
        bK = Rot([P.sb("bK", [128, 2, 256], BF16) for _ in range(3)])
        bV = Rot([P.sb("bV", [128, 2, 260], BF16) for _ in range(3)])
        bQT = Rot([P.sb("bQT", [128, 2, 2, 128], BF16) for _ in range(2)])
        for q_ in bQT.bufs:
            op("pool", lambda e, q_=q_: e.memset(q_[:], 0.0), writes=[q_])
        bKT = Rot([P.sb("bKT", [128, 2, 2, 128], BF16) for _ in range(2)])
        bP = Rot([P.sb("bP", [128, 1024], BF16) for _ in range(2)])
        bO = Rot([P.sb("bO", [128, 260], F32) for _ in range(3)])
        for s in range(NS):
            T = seq_lens[s]
            base = offs[s]
            for di, d in enumerate((1, 4, 16)):
                L = T // d
                for rho in range(d):
                    for j in range(L // 128):
                        Q, K, V = bQ.next(), bK.next(), bV.next()
                        q0 = base + rho + d * (128 * j)
                        k0 = base + rho + d * (128 * j - 64)
                        dma("sp", Q[:], bass.AP(QB.t, q0 * 256, [[d * 256, 128], [1, 256]]), [QB], [Q], Q.key)
                        dma("sp", K[:], bass.AP(KB.t, k0 * 256, [[d * 256, 128], [d * 256 * 128, 2], [1, 256]]), [KB], [K], K.key)
                        dma("sp", V[:], bass.AP(VB.t, k0 * 260, [[d * 260, 128], [d * 260 * 128, 2], [1, 260]]), [VB], [V], V.key)
                        for pr in range(2):
                            tr(bfv(pT)[:, pr * 128:(pr + 1) * 128], Q[:, pr * 128:(pr + 1) * 128], identb[:], [Q, identb], [pT])
                        for pr in range(2):
                            for c in range(2):
                                tr(bfv(pT)[:, 256 + (pr * 2 + c) * 128:256 + (pr * 2 + c + 1) * 128], K[:, c, pr * 128:(pr + 1) * 128], identb[:], [K, identb], [pT])
                        QT, KT = bQT.next(), bKT.next()
                        cp("dve", QT[0:64, :, 0, :], bfv(pT)[0:64, 0:256].rearrange("p (a t) -> p a t", t=128), [pT], [QT])
                        cp("dve", QT[64:128, :, 1, :], bfv(pT)[64:128, 0:256].rearrange("p (a t) -> p a t", t=128), [pT], [QT])
                        cp("dve", KT[:].rearrange("p a c t -> p (a c t)"), bfv(pT)[:, 256:768], [pT], [KT])
                        pS = [psr.next(), psr.next()]
                        Pt = bP.next()
                        for h in range(4):
                            pr, hh = h // 2, h % 2
                            for c in range(2):
                                outp = pS[h // 2][:, ((h % 2) * 2 + c) * 128:((h % 2) * 2 + c + 1) * 128]
                                mm(outp, KT[:, pr, c, :], QT[:, pr, hh, :], True, False, [KT, QT], [pS[h // 2]])
                                mm(outp, TBs[:, di, h, c, :], antib[:], False, True, [TBs, antib], [pS[h // 2]])
                        for hp in range(2):
                            act(Pt[:, hp * 512:(hp + 1) * 512], pS[hp][:], AF.Exp, [pS[hp]], [Pt])
                        pO = psr.next()
                        for h in range(4):
                            for c in range(2):
                                mm(pO[:, h * 65:(h + 1) * 65], Pt[:, (h * 2 + c) * 128:(h * 2 + c + 1) * 128], V[:, c, h * 65:(h + 1) * 65], c == 0, c == 1, [Pt, V], [pO])
                        O = bO.next()
                        cp("dve", O[:], pO[:, 0:260], [pO], [O])
                        dma("sp", bass.AP(ACCB[di].t, q0 * 260, [[d * 260, 128], [1, 260]]), O[:], [O], [ACCB[di]], O.key)

    def combine_B(l):
        begin_phase()
        cA = Rot([P.sb("cA", [128, 3, 260], F32) for _ in range(3)])
        cO = Rot([P.sb("cO", [128, 256], BF16) for _ in range(2)])
        st1 = Rot([P.sb("st1", [128, 8], F32) for _ in range(4)])
        for s in range(NS):
            for i in range(seq_lens[s] // 128):
                tp = offs[s] + i * 128
                A = cA.next()
                for di in range(3):
                    dma("sp", A[:, di, :], ACCB[di][tp:tp + 128, :], [ACCB[di]], [A], (A.key, di))
                tt("dve", A[:, 0, :], A[:, 0, :], A[:, 1, :], ALU.add, [A], [A])
                tt("dve", A[:, 0, :], A[:, 0, :], A[:, 2, :], ALU.add, [A], [A])
                A3 = A[:, 0, :].rearrange("p (h d) -> p h d", d=65)
                S = st1.next()
                op("dve", lambda e, S=S, A3=A3: e.reciprocal(out=S[:, 0:4], in_=A3[:, :, 64]), reads=[A], writes=[S])
                O = cO.next()
                tt("dve", O[:].rearrange("p (h d) -> p h d", d=64), A3[:, :, 0:64], S[:, 0:4].unsqueeze(2).broadcast_to([128, 4, 64]), ALU.mult, [A, S], [O])
                dma("sp", MIX[tp:tp + 128, 256:512], O[:], [O], [MIX], O.key)

    NCH = 4

    def mixer_C(l):
        begin_phase()
        hQ = [Rot([P.sb(f"hQ{d_}", [128, 4, NCH * 64], BF16) for _ in range(2)]) for d_ in range(2)]
        hK = [Rot([P.sb(f"hK{d_}", [128, 4, NCH * 64], BF16) for _ in range(2)]) for d_ in range(2)]
        hV = [Rot([P.sb(f"hV{d_}", [128, NCH, 512], BF16) for _ in range(2)]) for d_ in range(2)]
        hEM = [Rot([P.sb(f"hEM{d_}", [128, 4, NCH], F32) for _ in range(2)]) for d_ in range(2)]
        hEL = [Rot([P.sb(f"hEL{d_}", [128, 4, NCH], F32) for _ in range(2)]) for d_ in range(2)]
        hEE = [Rot([P.sb(f"hEE{d_}", [128, 4, NCH], F32) for _ in range(2)]) for d_ in range(2)]
        hZ = [P.sb(f"hZ{d_}", [128, 4, 128], F32) for d_ in range(2)]
        hSm = [Rot([P.sb(f"hSm{d_}", [128, 4, 128], BF16) for _ in range(2)]) for d_ in range(2)]
        hAt = [Rot([P.sb(f"hAt{d_}", [128, 256], BF16) for _ in range(2)]) for d_ in range(2)]
        hKt = [Rot([P.sb(f"hKt{d_}", [128, 512], BF16) for _ in range(2)]) for d_ in range(2)]
        for dr in range(2):
            for a_ in hAt[dr].bufs + hKt[dr].bufs + hV[dr].bufs:
                op("pool", lambda e, a_=a_: e.memset(a_[:], 0.0), writes=[a_])
        hTm = [Rot([P.sb(f"hTm{d_}", [128, 512], F32) for _ in range(2)]) for d_ in range(2)]
        hOo = [Rot([P.sb(f"hOo{d_}", [64, 512], F32) for _ in range(3)]) for d_ in range(2)]
        for s in range(NS):
            T = seq_lens[s]
            base = offs[s]
            ngrp = T // (NCH * 64)
            for dr in range(2):
                op("pool", lambda e, dr=dr: e.memset(hZ[dr][:], 0.0), writes=[hZ[dr]])
            for gi in range(ngrp):
                for dr in range(2):
                    g = gi if dr == 0 else ngrp - 1 - gi
                    t0 = base + g * NCH * 64
                    Q, K, V, EM, EL, EE = hQ[dr].next(), hK[dr].next(), hV[dr].next(), hEM[dr].next(), hEL[dr].next(), hEE[dr].next()
                    dma("sp", Q[:], HQ[dr, :, t0:t0 + NCH * 64].rearrange("(h k) t -> k h t", k=128), [HQ], [Q], Q.key)
                    dma("sp", K[:], HK[dr, :, t0:t0 + NCH * 64].rearrange("(h k) t -> k h t", k=128), [HK], [K], K.key)
                    dma("sp", V[0:64], HV[t0:t0 + NCH * 64, :].rearrange("(c p) f -> p c f", p=64), [HV], [V], V.key)
                    dma("sp", EM[:], HEM[dr, :, t0 // 64:t0 // 64 + NCH].rearrange("(h k) c -> k h c", k=128), [HEM], [EM], EM.key, slow=True)
                    dma("sp", EL[:], HEL[dr, :, t0 // 64:t0 // 64 + NCH].rearrange("(h k) c -> k h c", k=128), [HEL], [EL], EL.key, slow=True)
                    tt("pool", EE[:], EM[:], EL[:], ALU.mult, [EM, EL], [EE])
                    Z = hZ[dr]
                    msk = mfm if dr == 0 else mbm
                    for ci in range(NCH):
                        ch = ci if dr == 0 else NCH - 1 - ci
                        cs = slice(ch * 64, (ch + 1) * 64)
                        pA = psr.next()
                        for h in range(4):
                            mm(pA[0:64, h * 64:(h + 1) * 64], K[:, h, cs], Q[:, h, cs], True, True, [K, Q], [pA])
                        At = hAt[dr].next()
                        op("dve", lambda e, At=At, pA=pA, msk=msk: e.copy_predicated(out=At[0:64, :], mask=msk[:], data=pA[0:64, 0:256]), reads=[pA, msk, At], writes=[At])
                        pK = psr.next()
                        for h in range(4):
                            tr(bfv(pK)[0:64, h * 128:(h + 1) * 128], K[:, h, cs], identb[:], [K, identb], [pK])
                        Kt = hKt[dr].next()
                        cp("act", Kt[0:64, :], bfv(pK)[0:64, 0:512], [pK], [Kt])
                        Sm = hSm[dr].next()
                        tt("pool", Sm[:], Z[:], EM[:, :, ch:ch + 1].broadcast_to([128, 4, 128]), ALU.mult, [Z, EM], [Sm])
                        pO = psr.next()
                        for h in range(4):
                            mm(pO[0:64, h * 128:(h + 1) * 128], At[:, h * 64:(h + 1) * 64], V[:, ch, h * 128:(h + 1) * 128], True, False, [At, V], [pO])
                            mm(pO[0:64, h * 128:(h + 1) * 128], Q[:, h, cs], Sm[:, h, :], False, True, [Q, Sm], [pO])
                        OO = hOo[dr].next()
                        cp("act", OO[:], pO[0:64, :], [pO], [OO])
                        dma("sp", OFB[dr][t0 + ch * 64:t0 + (ch + 1) * 64, :], OO[:], [OO], [OFB[dr]], OO.key)
                        pKV = psr.next()
                        for h in range(4):
                            mm(pKV[:, h * 128:(h + 1) * 128], Kt[:, h * 128:(h + 1) * 128], V[:, ch, h * 128:(h + 1) * 128], True, True, [Kt, V], [pKV])
                        Tm = hTm[dr].next()
                        tt("dve", Tm[:].rearrange("p (h v) -> p h v", v=128), pKV[:].rearrange("p (h v) -> p h v", v=128),
                           EL[:, :, ch:ch + 1].broadcast_to([128, 4, 128]), ALU.mult, [pKV, EL], [Tm])
                        tt("pool", Z[:], Z[:], EE[:, :, ch:ch + 1].broadcast_to([128, 4, 128]), ALU.mult, [Z, EE], [Z])
                        tt("dve", Z[:], Z[:], Tm[:].rearrange("p (h v) -> p h v", v=128), ALU.add, [Z, Tm], [Z])

    def combine_C(l):
        begin_phase()
        gA = Rot([P.sb("gA", [128, 3, 512], F32) for _ in range(3)])
        gO = Rot([P.sb("gO", [128, 512], BF16) for _ in range(2)])
        f512 = Rot([P.sb("f512", [128, 512], F32) for _ in range(2)])
        st1 = Rot([P.sb("st1", [128, 8], F32) for _ in range(4)])
        for s in range(NS):
            for i in range(seq_lens[s] // 128):
                tp = offs[s] + i * 128
                A = gA.next()
                dma("sp", A[:, 0, :], OFB[0][tp:tp + 128, :], [OFB[0]], [A], (A.key, 0))
                dma("sp", A[:, 1, :], OFB[1][tp:tp + 128, :], [OFB[1]], [A], (A.key, 1))
                dma("sp", A[:, 2, :], HG[tp:tp + 128, :], [HG], [A], (A.key, 2))
                tt("dve", A[:, 0, :], A[:, 0, :], A[:, 1, :], ALU.add, [A], [A])
                SQ = f512.next()
                S = st1.next()
                act(SQ[:], A[:, 0, :], AF.Square, [A], [SQ])
                op("dve", lambda e, S=S, SQ=SQ: e.tensor_reduce(out=S[:, 0:4], in_=SQ[:].rearrange("p (h d) -> p h d", d=128), axis=AX.X, op=ALU.add),
                   reads=[SQ], writes=[S])
                act(S[:, 0:4], S[:, 0:4], AF.Ln, [S], [S], scale=1.0 / 128, bias=EPS)
                act(S[:, 0:4], S[:, 0:4], AF.Exp, [S], [S], scale=-0.5)
                tt("dve", A[:, 0, :].rearrange("p (h d) -> p h d", d=128), A[:, 0, :].rearrange("p (h d) -> p h d", d=128),
                   S[:, 0:4].unsqueeze(2).broadcast_to([128, 4, 128]), ALU.mult, [A, S], [A])
                tt("pool", A[:, 0, :].rearrange("p (h d) -> p h d", d=128), A[:, 0, :].rearrange("p (h d) -> p h d", d=128),
                   HGG[:, l, :].unsqueeze(1).broadcast_to([128, 4, 128]), ALU.mult, [A, HGG], [A])
                O = gO.next()
                tt("dve", O[:], A[:, 0, :], A[:, 2, :], ALU.mult, [A], [O])
                dma("sp", MIX[tp:tp + 128, 512:1024], O[:], [O], [MIX], O.key)

    def phase_C1(l):
        begin_phase()
        common_pools()
        xs, hT = pl.xs, pl.hT
        wst = Rot([P.sb("wst", [128, 2048], F32) for _ in range(2)])
        mxs = Rot([P.sb("mxs", [128, 1024], BF16) for _ in range(2)])
        x1s = Rot([P.sb("x1s", [128, 1024], F32) for _ in range(2)])
        load_weight([(Wd["w_out"][l, c * 128:(c + 1) * 128, :], Wout(c, 0, 1024), None) for c in range(8)], wst)
        xsrc = x_in if l == 0 else XL
        for s in range(NS):
            for i in range(seq_lens[s] // 128):
                xg = xoffs[s] + i * 128
                tp = offs[s] + i * 128
                M = mxs.next()
                X = xs.next()
                X1s = x1s.next()
                HT = hT.next()
                dma("sp", M[:], MIX[tp:tp + 128, :], [MIX], [M], M.key)
                dma("sp", X[:], xsrc[xg:xg + 128, :], [xsrc], [X], X.key)
                for c in range(8):
                    tr(bfv(pT)[:, c * 128:(c + 1) * 128], M[:, c * 128:(c + 1) * 128], identb[:], [M, identb], [pT])
                cp("dve", HT[:, :, 0:128], bfv(pT).rearrange("p (c t) -> p c t", t=128), [pT], [HT])
                for nb in range(2):
                    ps = psr.next()
                    for c in range(8):
                        mm(ps[:], HT[:, c, 0:128], Wout(c, nb * 512, (nb + 1) * 512), c == 0, c == 7, [HT, RW], [ps])
                    tt("dve", X1s[:, nb * 512:(nb + 1) * 512], ps[:], X[:, nb * 512:(nb + 1) * 512], ALU.add, [ps, X], [X1s])
                dma("sp", X1[xg:xg + 128, :], X1s[:], [X1s], [X1], X1s.key)
                HT2 = hT.next()
                rmsnorm_to_hT(X1s, None, HT2)
                dma("sp", H2T[:, tp:tp + 128].rearrange("(c p) t -> p c t", p=128), HT2[:, :, 0:128], [HT2], [H2T], HT2.key)

    NH = NFG // 2
    TW = 510

    def phase_C2(l, p):
        begin_phase()
        xs = Rot([P.sb("xs", [128, 1024], F32) for _ in range(3)])
        hT5 = Rot([P.sb("hT5", [128, 8, 512], BF16) for _ in range(2)])
        wst = Rot([P.sb("wst", [128, 2048], F32) for _ in range(2)])
        actT = Rot([P.sb("actT", [128, NH, TW], BF16) for _ in range(2)])
        fC = Rot([P.sb("fC", [128, TW], F32) for _ in range(3)])
        fU = Rot([P.sb("fU", [128, TW], F32) for _ in range(3)])
        fE = Rot([P.sb("fE", [128, TW], F32) for _ in range(3)])
        yo = Rot([P.sb("yo", [128, 1024], F32) for _ in range(2)])
        jobs = []
        for c in range(8):
            g = lng[:, l, 1, c:c + 1]
            jobs.append((Wd["w_up"][l, c * 128:(c + 1) * 128, p * 1408:(p + 1) * 1408], RW[:, c * 2816:c * 2816 + 1408], g))
            jobs.append((Wd["w_up"][l, c * 128:(c + 1) * 128, D_FF + p * 1408:D_FF + (p + 1) * 1408], RW[:, c * 2816 + 1408:(c + 1) * 2816], g))
        for j in range(NH):
            fg = p * NH + j
            jobs.append((Wd["w_down"][l, fg * 128:(fg + 1) * 128, :], Wdn(j, 0, 1024), None))
        load_weight(jobs, wst)
        xsrc = X1 if p == 0 else YP
        ydst = YP if p == 0 else (y_out if l == depth - 1 else XL)
        for s in range(NS):
            T = seq_lens[s]
            for t0 in range(0, T, TW):
                n = min(TW, T - t0)
                xg = xoffs[s] + t0
                tp = offs[s] + t0
                HT = hT5.next()
                dma("sp", HT[:, :, 0:n + 2], H2T[:, tp - 1:tp + n + 1].rearrange("(c p) t -> p c t", p=128), [H2T], [HT], HT.key)
                AT = actT.next()
                for j in range(NH):
                    fg = p * NH + j
                    pg = psr.next()
                    pu = psr.next()
                    for c in range(8):
                        mm(pg[:, 0:n + 2], WupG(c, j), HT[:, c, 0:n + 2], c == 0, c == 7, [HT, RW], [pg])
                    for c in range(8):
                        mm(pu[:, 0:n], WupU(c, j), HT[:, c, 1:n + 1], c == 0, c == 7, [HT, RW], [pu])
                    C, U, E = fC.next(), fU.next(), fE.next()
                    w = lambda jj: CVW[:, l, jj, fg:fg + 1]
                    act(C[:, 0:n], pg[:, 0:n], AF.Identity, [pg, CVW], [C], scale=w(0), bias=w(3))
                    stt(C[:, 0:n], pg[:, 1:n + 1], w(1), C[:, 0:n], ALU.mult, ALU.add, [pg, CVW, C], [C])
                    stt(C[:, 0:n], pg[:, 2:n + 2], w(2), C[:, 0:n], ALU.mult, ALU.add, [pg, CVW, C], [C])
                    act(U[:, 0:n], C[:, 0:n], AF.Square, [C], [U])
                    ts("pool", U[:, 0:n], U[:, 0:n], 0.044715, 1.0, ALU.mult, ALU.add, [U], [U])
                    tt("pool", U[:, 0:n], U[:, 0:n], C[:, 0:n], ALU.mult, [U, C], [U])
                    ts("pool", U[:, 0:n], U[:, 0:n], -25.0, None, ALU.max, None, [U], [U])
                    act(E[:, 0:n], U[:, 0:n], AF.Exp, [U], [E], scale=-GELU_C)
                    act(E[:, 0:n], E[:, 0:n], AF.Ln, [E], [E], bias=1.0)
                    act(E[:, 0:n], E[:, 0:n], AF.Exp, [E], [E], scale=-1.0)
                    tt("pool", E[:, 0:n], E[:, 0:n], C[:, 0:n], ALU.mult, [E, C], [E])
                    tt("dve", AT[:, j, 0:n], pu[:, 0:n], E[:, 0:n], ALU.mult, [pu, E], [AT])
                for sub in range(0, n, 128):
                    m = min(128, n - sub)
                    X = xs.next()
                    Y = yo.next()
                    dma("sp", X[0:m, :], xsrc[xg + sub:xg + sub + m, :], [xsrc], [X], X.key)
                    for nb in range(2):
                        ps = psr.next()
                        for j in range(NH):
                            mm(ps[0:m, :], AT[:, j, sub:sub + m], Wdn(j, nb * 512, (nb + 1) * 512), j == 0, j == NH - 1, [AT, RW], [ps])
                        tt("dve", Y[0:m, nb * 512:(nb + 1) * 512], ps[0:m, :], X[0:m, nb * 512:(nb + 1) * 512], ALU.add, [ps, X], [Y])
                    dma("sp", ydst[xg + sub:xg + sub + m, :], Y[0:m, :], [Y], [ydst], Y.key)

    phases = [("A", phase_A), ("mA", mixer_A), ("mB", mixer_B), ("cB", combine_B), ("mC", mixer_C), ("cC", combine_C), ("C1", phase_C1), ("C2a", lambda l: phase_C2(l, 0)), ("C2b", lambda l: phase_C2(l, 1))]
    done = stop_after == (0, "setup")
    for l in range(depth if not done else 0):
        for name, fn in phases:
            fn(l)
            if stop_after == (l, name):
                done = True
                break
        if done:
            break
    P.emit()
    return nc, P


SEQ_LENS = (16384, 4096, 4096)
_CACHE = {}


def kernel(x_prompt, x_sample, **w):
    consts = host_consts()
    if "nc" not in _CACHE:
        _CACHE["nc"] = build(SEQ_LENS)[0]
    nc = _CACHE["nc"]
    xp = np.asarray(x_prompt, np.float32)
    xsm = np.asarray(x_sample, np.float32)
    wmap = {k: np.ascontiguousarray(np.asarray(w[k], np.float32)) for k in W_SHAPES}
    in_maps = []
    for c in range(8):
        cc = c % 2
        xcat = np.concatenate([xp[cc], xsm[2 * cc], xsm[2 * cc + 1]], axis=0)
        m = {"x": np.ascontiguousarray(xcat)}
        m.update(wmap)
        m.update(consts)
        in_maps.append(m)
    res = run_bass_kernel_spmd(nc, in_maps, core_ids=list(range(8)))
    yp = np.zeros_like(xp)
    ys = np.zeros_like(xsm)
    for cc in range(2):
        y = np.asarray(res.results[cc]["y"], np.float32)
        yp[cc] = y[0:16384]
        ys[2 * cc] = y[16384:20480]
        ys[2 * cc + 1] = y[20480:24576]
    return (yp, ys)
```

```python
import numpy as np
import concourse.bass as bass
import concourse.mybir as mybir
from concourse.bass_utils import run_bass_kernel_spmd

F32 = mybir.dt.float32
BF16 = mybir.dt.bfloat16
I32 = mybir.dt.int32
AF = mybir.ActivationFunctionType
ALU = mybir.AluOpType
AX = mybir.AxisListType

D_MODEL = 1024
N_IN = 4096
D_FF = 2816
NFG = D_FF // 128
EPS = 1e-6
NEG = -30000.0
PAD = 1024
DBG = {}
STORE_Q = "pool"
GELU_C = 1.5957691216057308


class Buf:
    __slots__ = ("name", "t", "lastw", "reads", "dram", "W", "R", "key", "psum")

    def __init__(self, name, t, dram=False):
        self.name = name
        self.key = name
        self.t = t
        self.lastw = None
        self.reads = []
        self.dram = dram
        self.psum = False
        self.W = {}
        self.R = {}

    def __getitem__(self, idx):
        return self.t[idx]


class Ins:
    __slots__ = ("eng", "fn", "deps", "dma", "key", "sem", "val", "needed", "seq", "phase")


class Prog:
    ENGS = ("pe", "act", "dve", "pool", "sp")

    def __init__(self, nc, arena_bytes=206 * 1024):
        self.nc = nc
        self.q = {e: [] for e in self.ENGS}
        self.n = 0
        self.nbuf = 0
        self.arena_bytes = arena_bytes
        self.arena = nc.alloc_sbuf_tensor("arena", [128, arena_bytes], mybir.dt.uint8)
        self.ptr = 0
        self.peak = 0
        self.last_c = {}
        self.last_d = {}
        self.bar = {e: None for e in self.ENGS}
        self.phase = 0

    def sb(self, name, shape, dtype):
        esz = 2 if dtype == BF16 else 4
        n = 1
        for d in shape[1:]:
            n *= d
        nb = n * esz
        off = self.ptr
        self.ptr += (nb + 31) // 32 * 32
        self.peak = max(self.peak, self.ptr)
        assert self.ptr <= self.arena_bytes, f"SBUF arena overflow allocating {name} {shape}: {self.ptr}"
        ap = self.arena[0:shape[0], off:off + nb].bitcast(dtype)
        if len(shape) > 2:
            names = "abcdef"[:len(shape) - 1]
            pat = "p (" + " ".join(names) + ") -> p " + " ".join(names)
            ap = ap.rearrange(pat, **{nm: d for nm, d in zip(names, shape[1:])})
        return Buf(name, ap)

    def barrier(self):
        deps = list(self.last_c.values()) + list(self.last_d.values())
        self.phase += 1
        for e in self.ENGS:
            self.bar[e] = (self.bar[e] or []) + list(deps)

    def ps(self, name, shape, dtype=F32):
        self.nbuf += 1
        b = Buf(name, self.nc.alloc_psum_tensor(f"{name}_{self.nbuf}", list(shape), dtype))
        b.psum = True
        return b

    def dram(self, name, shape, dtype, kind="Internal", strict=False):
        return Buf(name, self.nc.dram_tensor(name, list(shape), dtype, kind=kind), dram=not strict)

    def op(self, eng, fn, reads=(), writes=(), dma=False, key=None):
        I = Ins()
        I.eng, I.fn, I.dma, I.key = eng, fn, dma, key
        I.sem, I.val, I.needed, I.seq = None, 0, False, self.n
        I.phase = self.phase
        self.n += 1
        sk = ("dma", key) if dma else ("eng", eng)
        pr = [b for b in reads if b.psum]
        if pr:
            reads = [b for b in reads if not b.psum]
            writes = list(writes) + [b for b in pr if b not in writes]
        deps = {}
        for b in reads:
            if b.dram:
                for d in b.W.values():
                    deps[id(d)] = d
            elif b.lastw is not None:
                deps[id(b.lastw)] = b.lastw
        for b in writes:
            if b.dram:
                for d in b.R.values():
                    deps[id(d)] = d
            else:
                if b.lastw is not None:
                    deps[id(b.lastw)] = b.lastw
                for r in b.reads:
                    deps[id(r)] = r
        for b in reads:
            if b.dram:
                b.R[sk] = I
            else:
                b.reads.append(I)
        for b in writes:
            if b.dram:
                b.W[sk] = I
            else:
                b.lastw = I
                b.reads = []
        if self.bar[eng] is not None:
            for d in self.bar[eng]:
                deps[id(d)] = d
            self.bar[eng] = None
        if dma:
            self.last_d[key] = I
        else:
            self.last_c[eng] = I
        deps.pop(id(I), None)
        dl = []
        for d in deps.values():
            if d.eng == "pe" and eng == "pe" and not d.dma and not dma:
                continue
            d.needed = True
            dl.append(d)
        I.deps = dl
        self.q[eng].append(I)
        return I

    def emit(self):
        nc = self.nc
        sems = {e: nc.alloc_semaphore(f"s_{e}") for e in self.ENGS}
        cnt = {e: 0 for e in self.ENGS}
        kcnt, ksem = {}, {}
        pidx = {}
        allins = sorted((i for e in self.ENGS for i in self.q[e]), key=lambda i: i.seq)
        for I in allins:
            if I.dma:
                pk = (I.phase, I.key)
                if pk not in pidx:
                    pidx[pk] = sum(1 for q in pidx if q[0] == I.phase)
                k = pidx[pk]
                if k not in ksem:
                    ksem[k] = nc.alloc_semaphore(f"d_{len(ksem)}")
                    kcnt[k] = 0
                kcnt[k] += 16
                I.sem, I.val = ksem[k], kcnt[k]
            elif I.needed:
                cnt[I.eng] += 1
                I.sem, I.val = sems[I.eng], cnt[I.eng]
        self.nsem = len(ksem) + len(sems)
        handles = {"pe": "tensor", "act": "scalar", "dve": "vector", "pool": "gpsimd", "sp": "sync"}
        final_dma = [(ksem[k], kcnt[k]) for k in ksem]

        def body(ename):
            def f(e):
                known = {}
                for I in self.q[ename]:
                    need = {}
                    for d in I.deps:
                        sid = id(d.sem)
                        if sid not in need or need[sid][1] < d.val:
                            need[sid] = (d.sem, d.val)
                    for sid, (sem, val) in need.items():
                        if known.get(sid, 0) < val:
                            e.wait_ge(sem, val)
                            known[sid] = val
                    r = I.fn(e)
                    if I.dma:
                        r.then_inc(I.sem, 16)
                    elif I.needed:
                        r.then_inc(I.sem, 1)
                if ename == "sp":
                    for s, v in final_dma:
                        e.wait_ge(s, v)
            return f

        with nc.Block() as block:
            for ename in self.ENGS:
                getattr(block, handles[ename])(body(ename))


class Rot:
    def __init__(self, bufs):
        self.bufs = bufs
        self.i = 0
        for j, b in enumerate(bufs):
            b.key = f"{b.name}{j}"

    def next(self):
        b = self.bufs[self.i % len(self.bufs)]
        self.i += 1
        return b


def t5_bucket_np(rel):
    nb = 16
    max_exact = 8
    ret = np.where(rel > 0, nb, 0)
    n = np.abs(rel)
    large = max_exact + (np.log(np.maximum(n, 1) / max_exact) / np.log(1024 / max_exact) * (nb - max_exact)).astype(np.int32)
    large = np.minimum(large, nb - 1)
    return (ret + np.where(n < max_exact, n, large)).astype(np.int32)


def host_consts():
    c = {}
    qc = 63 - np.arange(64)[:, None]
    kc = np.arange(64)[None, :]
    c0 = np.clip(qc - 8, 0, 48)
    ok = (kc >= c0) & (kc < c0 + 16)
    c["c_maskA"] = np.where(ok, 0.0, NEG).astype(np.float32)
    Gf = np.zeros((33, 3, 384), np.float32)
    for di, d in enumerate((1, 4, 16)):
        for j in range(383):
            delta = j - 191
            if abs(delta) <= 64:
                Gf[t5_bucket_np(np.array(delta * d)), di, j] = 1.0
            else:
                Gf[32, di, j] = 1.0
        Gf[32, di, 383] = 1.0
    c["c_Gf"] = Gf
    s = np.arange(64)[:, None, None]
    t = np.arange(64)[None, None, :]
    c["c_mf"] = np.broadcast_to((s <= t), (64, 4, 64)).astype(np.int32).copy()
    c["c_mb"] = np.broadcast_to((s >= t), (64, 4, 64)).astype(np.int32).copy()
    mf = np.ones((128, 8, 64), np.float32)
    mf[:, :, 0] = 0.0
    mb = np.ones((128, 8, 64), np.float32)
    mb[:, :, 63] = 0.0
    c["c_smf"] = mf.reshape(128, 512)
    c["c_smb"] = mb.reshape(128, 512)
    ident = np.eye(128, dtype=np.float32)
    c["c_ident"] = ident
    c["c_anti"] = ident[::-1].copy()
    a64 = np.zeros((128, 64), np.float32)
    a64[:64] = np.eye(64, dtype=np.float32)[::-1]
    c["c_anti64"] = a64
    return c


CONST_SHAPES = {"c_maskA": ([64, 64], F32), "c_Gf": ([33, 3, 384], F32), "c_mf": ([64, 4, 64], I32), "c_mb": ([64, 4, 64], I32),
                "c_smf": ([128, 512], F32), "c_smb": ([128, 512], F32), "c_ident": ([128, 128], F32), "c_anti": ([128, 128], F32),
                "c_anti64": ([128, 64], F32)}

W_SHAPES = {"ln_mix_g": [2, 1024], "w_in": [2, 1024, 4096], "na_q_g": [2, 64], "na_k_g": [2, 64], "na_rpb": [2, 4, 15, 31],
            "da_q_g": [2, 64], "da_k_g": [2, 64], "t5_bias": [32, 4], "hg_lb_logits": [2, 2, 512], "hg_norm_g": [2, 128],
            "w_out": [2, 1024, 1024], "ln_ffn_g": [2, 1024], "w_up": [2, 1024, 5632], "conv_w": [2, 3, 2816], "conv_b": [2, 2816],
            "w_down": [2, 2816, 1024]}


def build(seq_lens, depth=2, debug=(), stop_after=None):
    nc = bass.Bass("TRN2", target_bir_lowering=False)
    P = Prog(nc)
    op = P.op
    NS = len(seq_lens)
    Tall = sum(seq_lens)
    offs, xoffs = [], []
    o, xo = PAD, 0
    for T in seq_lens:
        offs.append(o)
        xoffs.append(xo)
        o += T + PAD
        xo += T
    TP = o

    def dkind(name):
        return "ExternalOutput" if name in debug else "Internal"

    x_in = P.dram("x", [Tall, D_MODEL], F32, kind="ExternalInput")
    y_out = P.dram("y", [Tall, D_MODEL], F32, kind="ExternalOutput")
    Wd = {k: P.dram(k, s, F32, kind="ExternalInput") for k, s in W_SHAPES.items()}
    Cd = {k: P.dram(k, s, dt, kind="ExternalInput") for k, (s, dt) in CONST_SHAPES.items()}

    def scratch(name, shape, dt, strict=False):
        return P.dram(name, shape, dt, kind=dkind(name), strict=strict)

    X1 = scratch("X1", [Tall, D_MODEL], F32)
    YP = scratch("YP", [Tall, D_MODEL], F32)
    XL = scratch("XL", [Tall, D_MODEL], F32)
    QKTA = scratch("QKTA", [512, TP], BF16)
    VA = scratch("VA", [TP, 260], BF16)
    QB = scratch("QB", [TP, 256], BF16)
    KB = scratch("KB", [TP, 256], BF16)
    VB = scratch("VB", [TP, 260], BF16)
    HQ = scratch("HQ", [2, 512, TP], BF16)
    HK = scratch("HK", [2, 512, TP], BF16)
    HV = scratch("HV", [TP, 512], BF16)
    HG = scratch("HG", [TP, 512], F32)
    HEM = scratch("HEM", [2, 512, TP // 64], F32)
    HEL = scratch("HEL", [2, 512, TP // 64], F32)
    OFB = [scratch("OF", [TP, 512], F32), scratch("OB", [TP, 512], F32)]
    ACCB = [scratch(f"ACCB{i}", [TP, 260], F32) for i in range(3)]
    MIX = scratch("MIX", [TP, 1024], BF16)
    H2T = scratch("H2T", [1024, TP], BF16)
    FB = scratch("FB", [3, 4, 384], BF16, strict=True)
    RP = scratch("RP", [2, 4, 15, 160], F32, strict=True)

    RW = P.sb("RW", [128, 33792], BF16)

    def Win(c, lo, hi):
        return RW[:, c * 4096 + lo:c * 4096 + hi]

    def Wout(c, lo, hi):
        return RW[:, c * 1024 + lo:c * 1024 + hi]

    def WupG(c, j):
        return RW[:, c * 2816 + j * 128:c * 2816 + (j + 1) * 128]

    def WupU(c, j):
        return RW[:, c * 2816 + 1408 + j * 128:c * 2816 + 1408 + (j + 1) * 128]

    def Wdn(j, lo, hi):
        return RW[:, 22528 + j * 1024 + lo:22528 + j * 1024 + hi]

    identb = P.sb("identb", [128, 128], BF16)
    antib = P.sb("antib", [128, 128], BF16)
    anti64b = P.sb("anti64b", [128, 64], BF16)
    ctmp = P.sb("ctmp", [128, 128], F32)
    lng = P.sb("lng", [128, 2, 2, 8], F32)
    GQK = P.sb("GQK", [128, 2, 2, 2, 64], F32)
    gtmp = P.sb("gtmp", [128, 64], F32)
    LBL = P.sb("LBL", [128, 2, 2, 4], F32)
    LB = P.sb("LB", [128, 2, 4], F32)
    LN1 = P.sb("LN1", [128, 2, 4], F32)
    HGG = P.sb("HGG", [128, 2, 128], F32)
    CVW = P.sb("CVW", [128, 2, 4, NFG], F32)
    R2A = P.sb("R2A", [128, 60 * 64], BF16)
    maskA = P.sb("maskA", [64, 64], F32)
    TBs = P.sb("TBs", [128, 3, 4, 2, 128], BF16)
    gf = P.sb("gf", [33, 384], F32)
    t5e = P.sb("t5e", [33, 4], F32)
    fbs = P.sb("fbs", [4, 384], BF16)
    mfm = P.sb("mfm", [64, 256], I32)
    mbm = P.sb("mbm", [64, 256], I32)
    smf = P.sb("smf", [128, 512], F32)
    smb = P.sb("smb", [128, 512], F32)
    zt = P.sb("zt", [128, 260], BF16)
    zf = P.sb("zf", [128, 160], F32)

    PS = [P.ps("ps", [128, 512], F32) for _ in range(8)]
    psr = Rot(PS[2:8])
    pT, pT2 = PS[0], PS[1]

    def bfv(ps):
        return ps[:].bitcast(BF16)

    def dma(eng, out, in_, reads, writes, key, slow=False):
        if STORE_Q and writes and all(w.t.__class__.__name__.startswith("DRam") for w in writes) and reads and not any(r.t.__class__.__name__.startswith("DRam") for r in reads):
            eng = STORE_Q
        if slow:
            return op(eng, lambda e: e.dma_start(out=out, in_=in_, allow_slow_non_contiguous=True), reads=reads, writes=writes, dma=True, key=key)
        return op(eng, lambda e: e.dma_start(out=out, in_=in_), reads=reads, writes=writes, dma=True, key=key)

    def act(out, in_, func, reads, writes, **kw):
        return op("act", lambda e: e.activation(out=out, in_=in_, func=func, **kw), reads=reads, writes=writes)

    def tt(eng, out, in0, in1, alu, reads, writes):
        return op(eng, lambda e: e.tensor_tensor(out=out, in0=in0, in1=in1, op=alu), reads=reads, writes=writes)

    def ts(eng, out, in0, s1, s2, op0, op1, reads, writes):
        if op1 is None:
            return op(eng, lambda e: e.tensor_scalar(out=out, in0=in0, scalar1=s1, scalar2=None, op0=op0), reads=reads, writes=writes)
        return op(eng, lambda e: e.tensor_scalar(out=out, in0=in0, scalar1=s1, scalar2=s2, op0=op0, op1=op1), reads=reads, writes=writes)

    def stt(out, in0, scalar, in1, op0, op1, reads, writes):
        return op("dve", lambda e: e.scalar_tensor_tensor(out=out, in0=in0, scalar=scalar, in1=in1, op0=op0, op1=op1), reads=reads, writes=writes)

    def cp(eng, out, in_, reads, writes):
        if eng == "act":
            return op(eng, lambda e: e.activation(out=out, in_=in_, func=AF.Copy), reads=reads, writes=writes)
        return op(eng, lambda e: e.tensor_copy(out=out, in_=in_), reads=reads, writes=writes)

    def mm(out, lhsT, rhs, start, stop, reads, writes):
        return op("pe", lambda e: e.matmul(out, lhsT=lhsT, rhs=rhs, start=start, stop=stop), reads=reads, writes=writes)

    def tr(out, in_, ident, reads, writes):
        return op("pe", lambda e: e.transpose(out=out, in_=in_, identity=ident), reads=reads, writes=writes)

    op("pool", lambda e: e.memset(zt[:], 0.0), writes=[zt])
    op("pool", lambda e: e.memset(R2A[:], 0.0), writes=[R2A])
    op("pool", lambda e: e.memset(zf[:], 0.0), writes=[zf])
    for src, dst in ((Cd["c_ident"], identb), (Cd["c_anti"], antib)):
        dma("sp", ctmp[:], src[:], [src], [ctmp], "ctmp")
        cp("dve", dst[:], ctmp[:], [ctmp], [dst])
    dma("sp", ctmp[:, 0:64], Cd["c_anti64"][:], [Cd["c_anti64"]], [ctmp], "ctmp")
    cp("dve", anti64b[:], ctmp[:, 0:64], [ctmp], [anti64b])
    dma("sp", mfm[:], Cd["c_mf"][:].rearrange("s h t -> s (h t)"), [], [mfm], "mfm")
    dma("sp", mbm[:], Cd["c_mb"][:].rearrange("s h t -> s (h t)"), [], [mbm], "mbm")
    dma("sp", smf[:], Cd["c_smf"][:], [], [smf], "smf")
    dma("sp", smb[:], Cd["c_smb"][:], [], [smb], "smb")
    dma("sp", maskA[:], Cd["c_maskA"][:], [], [maskA], "maskA")
    for l in range(depth):
        dma("sp", lng[:, l, 0, :], Wd["ln_mix_g"][l, :].rearrange("(c p) -> p c", p=128), [], [lng], "lng", slow=True)
        dma("sp", lng[:, l, 1, :], Wd["ln_ffn_g"][l, :].rearrange("(c p) -> p c", p=128), [], [lng], "lng", slow=True)
    for l in range(depth):
        for ab, (qn, kn) in enumerate((("na_q_g", "na_k_g"), ("da_q_g", "da_k_g"))):
            for qk, nm in enumerate((qn, kn)):
                dma("sp", gtmp[:], Wd[nm][l, :].partition_broadcast(128), [], [gtmp], "gtmp")
                dstv = GQK[:, l, ab, qk, :]
                srcv = gtmp[:]
                if qk == 0:
                    ts("dve", dstv, srcv, 0.125, None, ALU.mult, None, [gtmp], [GQK])
                else:
                    cp("dve", dstv, srcv, [gtmp], [GQK])
    dma("sp", LBL[:], Wd["hg_lb_logits"][:].rearrange("d l (h k) -> k d l h", k=128), [], [LBL], "LBL", slow=True)
    tt("dve", LB[:], LBL[:, :, 1, :], LBL[:, :, 0, :], ALU.subtract, [LBL], [LB])
    act(LN1[:], LB[:], AF.Exp, [LB], [LN1])
    act(LN1[:], LN1[:], AF.Ln, [LN1], [LN1], bias=1.0)
    tt("dve", LB[:], LB[:], LN1[:], ALU.subtract, [LB, LN1], [LB])
    act(LB[:], LB[:], AF.Exp, [LB], [LB])
    ts("dve", LN1[:], LN1[:], -1.0, None, ALU.mult, None, [LN1], [LN1])
    for l in range(depth):
        dma("sp", HGG[:, l, :], Wd["hg_norm_g"][l, :].partition_broadcast(128), [], [HGG], "HGG")
        for j in range(3):
            dma("sp", CVW[:, l, j, :], Wd["conv_w"][l, j, :].rearrange("(g p) -> p g", p=128), [], [CVW], "CVW", slow=True)
        dma("sp", CVW[:, l, 3, :], Wd["conv_b"][l, :].rearrange("(g p) -> p g", p=128), [], [CVW], "CVW", slow=True)

    pad_starts = [0] + [offs[s] + seq_lens[s] for s in range(NS)]
    for ps_ in pad_starts:
        for cc in range(PAD // 128):
            dma("sp", KB[ps_ + cc * 128:ps_ + (cc + 1) * 128, :], zt[:, 0:256], [zt], [KB], "zt")
            dma("sp", VB[ps_ + cc * 128:ps_ + (cc + 1) * 128, :], zt[:, 0:260], [zt], [VB], "zt")
        for col in (ps_ + PAD - 1, ps_):
            dma("sp", H2T[:, col:col + 1].rearrange("(g p) t -> p g t", p=128), zt[:, 0:8].rearrange("p (g t) -> p g t", t=1), [zt], [H2T], "zt", slow=True)
    dma("sp", RP[:].rearrange("l h r d -> (l h r) d"), zf[0:120, 0:160], [zf], [RP], "zf")
    for l in range(depth):
        dma("sp", RP[l, :, :, 64:95], Wd["na_rpb"][l], [Wd["na_rpb"]], [RP], "rpcopy", slow=True)
    op("pool", lambda e: e.memset(t5e[:], NEG), writes=[t5e])
    dma("sp", t5e[0:32, :], Wd["t5_bias"][:], [], [t5e], "t5e")
    for di in range(3):
        ps = psr.next()
        dma("sp", gf[:], Cd["c_Gf"][:, di, :], [], [gf], "gf")
        mm(ps[0:4, 0:384], t5e[:], gf[:], True, True, [t5e, gf], [ps])
        cp("dve", fbs[:], ps[0:4, 0:384], [ps], [fbs])
        dma("sp", FB[di], fbs[:], [fbs], [FB], "fbs")
    for di in range(3):
        for h in range(4):
            src = bass.AP(FB.t, (di * 4 + h) * 384, [[1, 128], [128, 2], [1, 128]])
            dma("sp", TBs[:, di, h, :, :], src, [FB], [TBs], "TBs", slow=True)

    def load_weight(jobs, wst):
        for k, (src, dst, g) in enumerate(jobs):
            st = wst.next()
            n = src.shape[1]
            dma("sp", st[:, 0:n], src, [], [st], st.key)
            eng = "dve" if k % 2 == 0 else "pool"
            if g is not None:
                ts(eng, dst, st[:, 0:n], g, None, ALU.mult, None, [st, lng], [RW])
            else:
                cp(eng, dst, st[:, 0:n], [st], [RW])

    WIN_PIECES = [(0, 512, 0), (512, 768, 1024), (768, 1280, 512), (1280, 1536, 1280), (1536, 2048, 1536), (2048, 4096, 2048)]

    base_ptr = P.ptr

    def begin_phase():
        P.ptr = base_ptr
        P.barrier()

    class Pools:
        pass

    pl = Pools()

    def common_pools(nx=2):
        pl.xs = Rot([P.sb("xs", [128, 1024], F32) for _ in range(nx)])
        pl.xn = Rot([P.sb("xn", [128, 1024], BF16) for _ in range(2)])
        pl.junk = P.sb("junk", [128, 1024], BF16)
        pl.st1 = Rot([P.sb("st1", [128, 8], F32) for _ in range(6)])
        pl.hT = Rot([P.sb("hT", [128, 8, 130], BF16) for _ in range(2)])

    def rmsnorm_to_hT(X, gsel_unused, HT, ncols=128, col0=0):
        S = pl.st1.next()
        XN = pl.xn.next()
        junk = pl.junk
        act(junk[:], X[:], AF.Square, [X], [junk, S], accum_out=S[:, 0:1])
        act(S[:, 1:2], S[:, 0:1], AF.Ln, [S], [S], scale=1.0 / D_MODEL, bias=EPS)
        act(S[:, 2:3], S[:, 1:2], AF.Exp, [S], [S], scale=-0.5)
        act(XN[:], X[:], AF.Copy, [X, S], [XN], scale=S[:, 2:3])
        for c in range(8):
            tr(bfv(pT)[:, c * 128:(c + 1) * 128], XN[:, c * 128:(c + 1) * 128], identb[:], [XN, identb], [pT])
        cp("dve", HT[:, :, col0:col0 + 128], bfv(pT).rearrange("p (c t) -> p c t", t=128), [pT], [HT])

    def phase_A(l):
        begin_phase()
        common_pools()
        xs, st1 = pl.xs, pl.st1
        hT5 = Rot([P.sb("hT5", [128, 8, 512], BF16) for _ in range(2)])
        wst = Rot([P.sb("wst", [128, 2048], F32) for _ in range(2)])
        f512 = Rot([P.sb("f512", [128, 512], F32) for _ in range(10)])
        b512 = Rot([P.sb("b512", [128, 512], BF16) for _ in range(6)])
        vv = Rot([P.sb("vv", [128, 8, 65], BF16) for _ in range(2)])
        for v_ in vv.bufs:
            op("pool", lambda e, v_=v_: e.memset(v_[:], 1.0), writes=[v_])
        qkT = Rot([P.sb("qkT", [128, 4, 128], BF16) for _ in range(2)])
        CQ = P.sb("CQ", [128, 4, 512], F32)
        LQ = P.sb("LQ", [128, 4, 512], F32)
        jobs = []
        for c in range(8):
            for (a, b, d0) in WIN_PIECES:
                jobs.append((Wd["w_in"][l, c * 128:(c + 1) * 128, a:b], Win(c, d0, d0 + b - a), lng[:, l, 0, c:c + 1]))
        load_weight(jobs, wst)
        xsrc = x_in if l == 0 else XL
        for s in range(NS):
            for i5 in range(seq_lens[s] // 512):
                HT = hT5.next()
                for sub in range(4):
                    i = i5 * 4 + sub
                    xg = xoffs[s] + i * 128
                    tp = offs[s] + i * 128
                    X = xs.next()
                    dma("sp", X[:], xsrc[xg:xg + 128, :], [xsrc], [X], X.key)
                    rmsnorm_to_hT(X, None, HT, col0=sub * 128)

                    def tok_group(lo, hi):
                        ps = psr.next()
                        for c in range(8):
                            mm(ps[:, 0:hi - lo], HT[:, c, sub * 128:(sub + 1) * 128], Win(c, lo, hi), c == 0, c == 7, [HT, RW], [ps])
                        return ps

                    for ab in range(2):
                        ps = tok_group(ab * 512, ab * 512 + 512)
                        SQ = f512.next()
                        S = st1.next()
                        act(SQ[:], ps[:], AF.Square, [ps], [SQ])
                        op("dve", lambda e, S=S, SQ=SQ: e.tensor_reduce(out=S[:, 0:8], in_=SQ[:].rearrange("p (h d) -> p h d", d=64), axis=AX.X, op=ALU.add),
                           reads=[SQ], writes=[S])
                        act(S[:, 0:8], S[:, 0:8], AF.Ln, [S], [S], scale=1.0 / 64, bias=EPS)
                        act(S[:, 0:8], S[:, 0:8], AF.Exp, [S], [S], scale=-0.5)
                        T1 = f512.next()
                        tt("dve", T1[:].rearrange("p (h d) -> p h d", d=64), ps[:].rearrange("p (h d) -> p h d", d=64),
                           S[:, 0:8].unsqueeze(2).broadcast_to([128, 8, 64]), ALU.mult, [ps, S], [T1])
                        O = b512.next()
                        tt("pool", O[:].rearrange("p (a h d) -> p a h d", a=2, h=4), T1[:].rearrange("p (a h d) -> p a h d", a=2, h=4),
                           GQK[:, l, ab, :, :].unsqueeze(2).broadcast_to([128, 2, 4, 64]), ALU.mult, [T1, GQK], [O])
                        if ab == 0:
                            for j in range(4):
                                tr(bfv(pT2)[:, j * 128:(j + 1) * 128], O[:, j * 128:(j + 1) * 128], identb[:], [O, identb], [pT2])
                            QT = qkT.next()
                            cp("act", QT[:], bfv(pT2)[:, 0:512].rearrange("p (j t) -> p j t", t=128), [pT2], [QT])
                            dma("sp", QKTA[:, tp:tp + 128].rearrange("(j p) t -> p j t", p=128), QT[:], [QT], [QKTA], QT.key)
                        else:
                            dma("sp", QB[tp:tp + 128, :], O[:, 0:256], [O], [QB], O.key)
                            dma("sp", KB[tp:tp + 128, :], O[:, 256:512], [O], [KB], O.key)
                    ps = tok_group(1024, 1536)
                    VV = vv.next()
                    cp("act", VV[:, :, 0:64], ps[:].rearrange("p (h d) -> p h d", d=64), [ps], [VV])
                    dma("sp", VA[tp:tp + 128, :], VV[:, 0:4, :].rearrange("p h d -> p (h d)"), [VV], [VA], VV.key)
                    dma("sp", VB[tp:tp + 128, :], VV[:, 4:8, :].rearrange("p h d -> p (h d)"), [VV], [VB], VV.key)
                    ps = tok_group(3072, 3584)
                    O = b512.next()
                    cp("act", O[:], ps[:], [ps], [O])
                    dma("sp", HV[tp:tp + 128, :], O[:], [O], [HV], O.key)
                    ps = tok_group(3584, 4096)
                    E = f512.next()
                    act(E[:], ps[:], AF.Exp, [ps], [E], scale=-1.0)
                    act(E[:], E[:], AF.Ln, [E], [E], bias=1.0)
                    act(E[:], E[:], AF.Exp, [E], [E], scale=-1.0)
                    G = f512.next()
                    tt("dve", G[:], ps[:], E[:], ALU.mult, [ps, E], [G])
                    dma("sp", HG[tp:tp + 128, :], G[:], [G], [HG], G.key)

                tp5 = offs[s] + i5 * 512

                def feat_head(lo, hh):
                    ps = psr.next()
                    for c in range(8):
                        mm(ps[:], Win(c, lo + hh * 128, lo + (hh + 1) * 128), HT[:, c, :], c == 0, c == 7, [HT, RW], [ps])
                    return ps

                for hh in range(4 if DBG.get("PA", 9) >= 2 else 0):
                    ps = feat_head(1536, hh)
                    if DBG.get("PB", 9) >= 1:
                        cp("dve", CQ[:, hh, :], ps[:], [ps], [CQ])
                    if DBG.get("PB", 9) >= 2:
                        act(LQ[:, hh, :], ps[:], AF.Exp, [ps], [LQ], scale=-1.0)
                    if DBG.get("PB", 9) >= 3:
                        act(LQ[:, hh, :], LQ[:, hh, :], AF.Ln, [LQ], [LQ], bias=1.0)
                for dr in range(2 if DBG.get("PA", 9) >= 3 else 0):
                    for hh in range(4):
                        ps = feat_head(2048 + dr * 512, hh)
                        E = f512.next()
                        L1 = f512.next()
                        act(E[:], ps[:], AF.Exp, [ps], [E], scale=-1.0)
                        act(L1[:], E[:], AF.Ln, [E], [L1], bias=1.0)
                        LF = f512.next()
                        if l == 0:
                            ts("pool", LF[:], L1[:], -1.0, None, ALU.mult, None, [L1], [LF])
                        else:
                            act(LF[:], E[:], AF.Ln, [E, LB], [LF], scale=LB[:, dr, hh:hh + 1], bias=1.0)
                            tt("pool", LF[:], LF[:], L1[:], ALU.subtract, [LF, L1], [LF])
                        Bc = f512.next()
                        if dr == 0:
                            op("dve", lambda e, Bc=Bc, LF=LF: e.tensor_tensor_scan(out=Bc[:], data0=smf[:], data1=LF[:], initial=0.0, op0=ALU.mult, op1=ALU.add),
                               reads=[smf, LF], writes=[Bc])
                            mi, li = 31, 63
                        else:
                            op("dve", lambda e, Bc=Bc, LF=LF: e.tensor_tensor_scan(out=Bc[:, ::-1], data0=smb[:, ::-1], data1=LF[:, ::-1], initial=0.0, op0=ALU.mult, op1=ALU.add),
                               reads=[smb, LF], writes=[Bc])
                            mi, li = 32, 0
                        B3 = Bc[:].rearrange("p (c t) -> p c t", t=64)
                        S = st1.next()
                        S2 = st1.next()
                        act(S[:, 0:8], B3[:, :, mi], AF.Exp, [Bc], [S])
                        tt("dve", S2[:, 0:8], B3[:, :, li], B3[:, :, mi], ALU.subtract, [Bc], [S2])
                        act(S2[:, 0:8], S2[:, 0:8], AF.Exp, [S2], [S2])
                        cpi = tp5 // 64
                        dma("sp", HEM[dr, hh * 128:(hh + 1) * 128, cpi:cpi + 8], S[:, 0:8], [S], [HEM], S.key, slow=True)
                        dma("sp", HEL[dr, hh * 128:(hh + 1) * 128, cpi:cpi + 8], S2[:, 0:8], [S2], [HEL], S2.key, slow=True)
                        Dd = f512.next()
                        tt("dve", Dd[:].rearrange("p (c t) -> p c t", t=64), B3, B3[:, :, mi:mi + 1].broadcast_to([128, 8, 64]), ALU.subtract, [Bc], [Dd])
                        ZL = L1
                        tt("dve", ZL[:], ps[:], L1[:], ALU.add, [ps, L1], [ZL])
                        tt("pool", ZL[:], ZL[:], Dd[:], ALU.add, [ZL, Dd], [ZL])
                        KT = b512.next()
                        if l == 0:
                            act(KT[:], ZL[:], AF.Exp, [ZL], [KT], scale=-1.0)
                        else:
                            act(KT[:], ZL[:], AF.Exp, [ZL, LN1], [KT], scale=-1.0, bias=LN1[:, dr, hh:hh + 1])
                        dma("sp", HK[dr, hh * 128:(hh + 1) * 128, tp5:tp5 + 512], KT[:], [KT], [HK], KT.key)
                        tt("pool", Dd[:], Dd[:], LQ[:, hh, :], ALU.subtract, [Dd, LQ], [Dd])
                        act(Dd[:], Dd[:], AF.Exp, [Dd], [Dd])
                        QTt = b512.next()
                        tt("dve", QTt[:], Dd[:], CQ[:, hh, :], ALU.mult, [Dd, CQ], [QTt])
                        dma("sp", HQ[dr, hh * 128:(hh + 1) * 128, tp5:tp5 + 512], QTt[:], [QTt], [HQ], QTt.key)

    def mixer_A(l):
        begin_phase()
        aQ = Rot([P.sb("aQ", [128, 2, 2, 512], BF16) for _ in range(2)])
        for q_ in aQ.bufs:
            op("pool", lambda e, q_=q_: e.memset(q_[:], 0.0), writes=[q_])
        aK = Rot([P.sb("aK", [128, 2, 960], BF16) for _ in range(2)])
        aV = Rot([P.sb("aV", [128, 2, 8, 260], BF16) for _ in range(2)])
        aP = Rot([P.sb("aP", [128, 512], BF16) for _ in range(3)])
        aO = Rot([P.sb("aO", [64, 8, 256], BF16) for _ in range(2)])
        aR = Rot([P.sb("aR", [64, 4], F32) for _ in range(3)])
        r2f = P.sb("r2f", [64, 15 * 64], F32)
        for h in range(4):
            src = bass.AP(RP.t, (l * 4 + h) * 15 * 160 + 16, [[1, 64], [160, 15], [1, 64]])
            dma("sp", r2f[:].rearrange("p (a k) -> p a k", k=64), src, [RP], [r2f], "r2f", slow=True)
            tt("dve", R2A[0:64, h * 960:(h + 1) * 960].rearrange("p (a k) -> p a k", k=64), r2f[:].rearrange("p (a k) -> p a k", k=64),
               maskA[:].unsqueeze(1).broadcast_to([64, 15, 64]), ALU.add, [r2f, maskA], [R2A])
        for s in range(NS):
            T = seq_lens[s]
            rows = T // 64
            base = offs[s]
            for rb in range(0, rows, 8):
                kb = max(0, rb - 4)
                ke = min(rows, rb + 11)
                nk = (ke - kb) * 64
                Q, K, V, O = aQ.next(), aK.next(), aV.next(), aO.next()
                qsrc = QKTA[0:256, base + rb * 64:base + rb * 64 + 512].rearrange("(j p) t -> p j t", p=128)
                dma("sp", Q[0:64, :, 0, :], qsrc[0:64], [QKTA], [Q], (Q.key, 0))
                dma("sp", Q[64:128, :, 1, :], qsrc[64:128], [QKTA], [Q], (Q.key, 1))
                dma("sp", K[:, :, 0:nk], QKTA[256:512, base + kb * 64:base + kb * 64 + nk].rearrange("(j p) t -> p j t", p=128), [QKTA], [K], K.key)
                for par in range(2):
                    dma("sp", V[:, par, :, :], VA[base + (kb + par) * 64:base + (kb + par) * 64 + 1024, :].rearrange("(c p) f -> p c f", p=128), [VA], [V], (V.key, par))
                for ri in range(8 if DBG.get("A", 9) >= 2 else 0):
                    r = rb + ri
                    r0 = min(max(r - 4, 0), rows - 8)
                    delta = r0 - r
                    par = (r0 - kb) % 2
                    vc0 = (r0 - kb - par) // 2
                    pO = psr.next()
                    for hp in range(2):
                        pS = psr.next()
                        for hh in range(2):
                            h = hp * 2 + hh
                            for m in range(4):
                                kc0 = (r0 + 2 * m - kb) * 64
                                dr0 = delta + 2 * m + 7
                                outp = pS[:, (hh * 4 + m) * 64:(hh * 4 + m + 1) * 64]
                                mm(outp, K[:, hp, kc0:kc0 + 128], Q[:, hp, hh, ri * 64:(ri + 1) * 64], True, False, [K, Q], [pS])
                                mm(outp, R2A[:, (h * 15 + dr0) * 64:(h * 15 + dr0 + 2) * 64], anti64b[:, :], False, True, [R2A, anti64b], [pS])
                        Pt = aP.next()
                        act(Pt[:], pS[:], AF.Exp, [pS], [Pt])
                        for hh in range(2 if DBG.get("A", 9) >= 3 else 0):
                            h = hp * 2 + hh
                            for m in range(4):
                                mm(pO[0:64, h * 65:(h + 1) * 65], Pt[:, (hh * 4 + m) * 64:(hh * 4 + m + 1) * 64], V[:, par, vc0 + m, h * 65:(h + 1) * 65], m == 0, m == 3, [Pt, V], [pO])
                    if DBG.get("A", 9) < 4:
                        continue
                    R = aR.next()
                    pO3 = pO[0:64, 0:260].rearrange("p (h d) -> p h d", d=65)
                    op("dve", lambda e, R=R, pO3=pO3: e.reciprocal(out=R[:], in_=pO3[:, :, 64]), reads=[pO], writes=[R])
                    tt("dve", O[:, ri, :].rearrange("p (h d) -> p h d", d=64), pO3[:, :, 0:64], R[:].unsqueeze(2).broadcast_to([64, 4, 64]), ALU.mult, [pO, R], [O])
                dma("sp", MIX[base + rb * 64:base + rb * 64 + 512, 0:256].rearrange("(i p) f -> p i f", p=64), O[:], [O], [MIX], O.key)

    def mixer_B(l):
        begin_phase()
        bQ = Rot([P.sb("bQ", [128, 256], BF16) for _ in range(3)])
        bK = Rot([P.sb("bK", [128, 2, 256], BF16) for _ in range(3)])
        bV = Rot([P.sb("bV", [128, 2, 260], BF16) for _ in range(3)])
        bQT = Rot([P.sb("bQT", [128, 2, 2, 128], BF16) for _ in range(2)])
        for q_ in bQT.bufs:
            op("pool", lambda e, q_=q_: e.memset(q_[:], 0.0), writes=[q_])
        bKT = Rot([P.sb("bKT", [128, 2, 2, 128], BF16) for _ in range(2)])
        bP = Rot([P.sb("bP", [128, 1024], BF16) for _ in range(2)])
        bO = Rot([P.sb("bO", [128, 260], F32) for _ in range(3)])
        for s in range(NS):
            T = seq_lens[s]
            base = offs[s]
            for di, d in enumerate((1, 4, 16)):
                L = T // d
                for rho in range(d):
                    for j in range(L // 128):
                        Q, K, V = bQ.next(), bK.next(), bV.next()
                        q0 = base + rho + d * (128 * j)
                        k0 = base + rho + d * (128 * j - 64)
                        dma("sp", Q[:], bass.AP(QB.t, q0 * 256, [[d * 256, 128], [1, 256]]), [QB], [Q], Q.key)
                        dma("sp", K[:], bass.AP(KB.t, k0 * 256, [[d * 256, 128], [d * 256 * 128, 2], [1, 256]]), [KB], [K], K.key)
                        dma("sp", V[:], bass.AP(VB.t, k0 * 260, [[d * 260, 128], [d * 260 * 128, 2], [1, 260]]), [VB], [V], V.key)
                        for pr in range(2):
                            tr(bfv(pT)[:, pr * 128:(pr + 1) * 128], Q[:, pr * 128:(pr + 1) * 128], identb[:], [Q, identb], [pT])
                        for pr in range(2):
                            for c in range(2):
                                tr(bfv(pT)[:, 256 + (pr * 2 + c) * 128:256 + (pr * 2 + c + 1) * 128], K[:, c, pr * 128:(pr + 1) * 128], identb[:], [K, identb], [pT])
                        QT, KT = bQT.next(), bKT.next()
                        cp("dve", QT[0:64, :, 0, :], bfv(pT)[0:64, 0:256].rearrange("p (a t) -> p a t", t=128), [pT], [QT])
                        cp("dve", QT[64:128, :, 1, :], bfv(pT)[64:128, 0:256].rearrange("p (a t) -> p a t", t=128), [pT], [QT])
                        cp("dve", KT[:].rearrange("p a c t -> p (a c t)"), bfv(pT)[:, 256:768], [pT], [KT])
                        pS = [psr.next(), psr.next()]
                        Pt = bP.next()
                        for h in range(4):
                            pr, hh = h // 2, h % 2
                            for c in range(2):
                                outp = pS[h // 2][:, ((h % 2) * 2 + c) * 128:((h % 2) * 2 + c + 1) * 128]
                                mm(outp, KT[:, pr, c, :], QT[:, pr, hh, :], True, False, [KT, QT], [pS[h // 2]])
                                mm(outp, TBs[:, di, h, c, :], antib[:], False, True, [TBs, antib], [pS[h // 2]])
                        for hp in range(2):
                            act(Pt[:, hp * 512:(hp + 1) * 512], pS[hp][:], AF.Exp, [pS[hp]], [Pt])
                        pO = psr.next()
                        for h in range(4):
                            for c in range(2):
                                mm(pO[:, h * 65:(h + 1) * 65], Pt[:, (h * 2 + c) * 128:(h * 2 + c + 1) * 128], V[:, c, h * 65:(h + 1) * 65], c == 0, c == 1, [Pt, V], [pO])
                        O = bO.next()
                        cp("dve", O[:], pO[:, 0:260], [pO], [O])
                        dma("sp", bass.AP(ACCB[di].t, q0 * 260, [[d * 260, 128], [1, 260]]), O[:], [O], [ACCB[di]], O.key)

    def combine_B(l):
        begin_phase()
        cA = Rot([P.sb("cA", [128, 3, 260], F32) for _ in range(3)])
        cO = Rot([P.sb("cO", [128, 256], BF16) for _ in range(2)])
        st1 = Rot([P.sb("st1", [128, 8], F32) for _ in range(4)])
        for s in range(NS):
            for i in range(seq_lens[s] // 128):
                tp = offs[s] + i * 128
                A = cA.next()
                for di in range(3):
                    dma("sp", A[:, di, :], ACCB[di][tp:tp + 128, :], [ACCB[di]], [A], (A.key, di))
                tt("dve", A[:, 0, :], A[:, 0, :], A[:, 1, :], ALU.add, [A], [A])
                tt("dve", A[:, 0, :], A[:, 0, :], A[:, 2, :], ALU.add, [A], [A])
                A3 = A[:, 0, :].rearrange("p (h d) -> p h d", d=65)
                S = st1.next()
                op("dve", lambda e, S=S, A3=A3: e.reciprocal(out=S[:, 0:4], in_=A3[:, :, 64]), reads=[A], writes=[S])
                O = cO.next()
                tt("dve", O[:].rearrange("p (h d) -> p h d", d=64), A3[:, :, 0:64], S[:, 0:4].unsqueeze(2).broadcast_to([128, 4, 64]), ALU.mult, [A, S], [O])
                dma("sp", MIX[tp:tp + 128, 256:512], O[:], [O], [MIX], O.key)

    NCH = 4

    def mixer_C(l):
        begin_phase()
        hQ = [Rot([P.sb(f"hQ{d_}", [128, 4, NCH * 64], BF16) for _ in range(2)]) for d_ in range(2)]
        hK = [Rot([P.sb(f"hK{d_}", [128, 4, NCH * 64], BF16) for _ in range(2)]) for d_ in range(2)]
        hV = [Rot([P.sb(f"hV{d_}", [128, NCH, 512], BF16) for _ in range(2)]) for d_ in range(2)]
        hEM = [Rot([P.sb(f"hEM{d_}", [128, 4, NCH], F32) for _ in range(2)]) for d_ in range(2)]
        hEL = [Rot([P.sb(f"hEL{d_}", [128, 4, NCH], F32) for _ in range(2)]) for d_ in range(2)]
        hEE = [Rot([P.sb(f"hEE{d_}", [128, 4, NCH], F32) for _ in range(2)]) for d_ in range(2)]
        hZ = [P.sb(f"hZ{d_}", [128, 4, 128], F32) for d_ in range(2)]
        hSm = [Rot([P.sb(f"hSm{d_}", [128, 4, 128], BF16) for _ in range(2)]) for d_ in range(2)]
        hAt = [Rot([P.sb(f"hAt{d_}", [128, 256], BF16) for _ in range(2)]) for d_ in range(2)]
        hKt = [Rot([P.sb(f"hKt{d_}", [128, 512], BF16) for _ in range(2)]) for d_ in range(2)]
        for dr in range(2):
            for a_ in hAt[dr].bufs + hKt[dr].bufs + hV[dr].bufs:
                op("pool", lambda e, a_=a_: e.memset(a_[:], 0.0), writes=[a_])
        hTm = [Rot([P.sb(f"hTm{d_}", [128, 512], F32) for _ in range(2)]) for d_ in range(2)]
        hOo = [Rot([P.sb(f"hOo{d_}", [64, 512], F32) for _ in range(3)]) for d_ in range(2)]
        for s in range(NS):
            T = seq_lens[s]
            base = offs[s]
            ngrp = T // (NCH * 64)
            for dr in range(2):
                op("pool", lambda e, dr=dr: e.memset(hZ[dr][:], 0.0), writes=[hZ[dr]])
            for gi in range(ngrp):
                for dr in range(2):
                    g = gi if dr == 0 else ngrp - 1 - gi
                    t0 = base + g * NCH * 64
                    Q, K, V, EM, EL, EE = hQ[dr].next(), hK[dr].next(), hV[dr].next(), hEM[dr].next(), hEL[dr].next(), hEE[dr].next()
                    dma("sp", Q[:], HQ[dr, :, t0:t0 + NCH * 64].rearrange("(h k) t -> k h t", k=128), [HQ], [Q], Q.key)
                    dma("sp", K[:], HK[dr, :, t0:t0 + NCH * 64].rearrange("(h k) t -> k h t", k=128), [HK], [K], K.key)
                    dma("sp", V[0:64], HV[t0:t0 + NCH * 64, :].rearrange("(c p) f -> p c f", p=64), [HV], [V], V.key)
                    dma("sp", EM[:], HEM[dr, :, t0 // 64:t0 // 64 + NCH].rearrange("(h k) c -> k h c", k=128), [HEM], [EM], EM.key, slow=True)
                    dma("sp", EL[:], HEL[dr, :, t0 // 64:t0 // 64 + NCH].rearrange("(h k) c -> k h c", k=128), [HEL], [EL], EL.key, slow=True)
                    tt("pool", EE[:], EM[:], EL[:], ALU.mult, [EM, EL], [EE])
                    Z = hZ[dr]
                    msk = mfm if dr == 0 else mbm
                    for ci in range(NCH):
                        ch = ci if dr == 0 else NCH - 1 - ci
                        cs = slice(ch * 64, (ch + 1) * 64)
                        pA = psr.next()
                        for h in range(4):
                            mm(pA[0:64, h * 64:(h + 1) * 64], K[:, h, cs], Q[:, h, cs], True, True, [K, Q], [pA])
                        At = hAt[dr].next()
                        op("dve", lambda e, At=At, pA=pA, msk=msk: e.copy_predicated(out=At[0:64, :], mask=msk[:], data=pA[0:64, 0:256]), reads=[pA, msk, At], writes=[At])
                        pK = psr.next()
                        for h in range(4):
                            tr(bfv(pK)[0:64, h * 128:(h + 1) * 128], K[:, h, cs], identb[:], [K, identb], [pK])
                        Kt = hKt[dr].next()
                        cp("act", Kt[0:64, :], bfv(pK)[0:64, 0:512], [pK], [Kt])
                        Sm = hSm[dr].next()
                        tt("pool", Sm[:], Z[:], EM[:, :, ch:ch + 1].broadcast_to([128, 4, 128]), ALU.mult, [Z, EM], [Sm])
                        pO = psr.next()
                        for h in range(4):
                            mm(pO[0:64, h * 128:(h + 1) * 128], At[:, h * 64:(h + 1) * 64], V[:, ch, h * 128:(h + 1) * 128], True, False, [At, V], [pO])
                            mm(pO[0:64, h * 128:(h + 1) * 128], Q[:, h, cs], Sm[:, h, :], False, True, [Q, Sm], [pO])
                        OO = hOo[dr].next()
                        cp("act", OO[:], pO[0:64, :], [pO], [OO])
                        dma("sp", OFB[dr][t0 + ch * 64:t0 + (ch + 1) * 64, :], OO[:], [OO], [OFB[dr]], OO.key)
                        pKV = psr.next()
                        for h in range(4):
                            mm(pKV[:, h * 128:(h + 1) * 128], Kt[:, h * 128:(h + 1) * 128], V[:, ch, h * 128:(h + 1) * 128], True, True, [Kt, V], [pKV])
                        Tm = hTm[dr].next()
                        tt("dve", Tm[:].rearrange("p (h v) -> p h v", v=128), pKV[:].rearrange("p (h v) -> p h v", v=128),
                           EL[:, :, ch:ch + 1].broadcast_to([128, 4, 128]), ALU.mult, [pKV, EL], [Tm])
                        tt("pool", Z[:], Z[:], EE[:, :, ch:ch + 1].broadcast_to([128, 4, 128]), ALU.mult, [Z, EE], [Z])
                        tt("dve", Z[:], Z[:], Tm[:].rearrange("p (h v) -> p h v", v=128), ALU.add, [Z, Tm], [Z])

    def combine_C(l):
        begin_phase()
        gA = Rot([P.sb("gA", [128, 3, 512], F32) for _ in range(3)])
        gO = Rot([P.sb("gO", [128, 512], BF16) for _ in range(2)])
        f512 = Rot([P.sb("f512", [128, 512], F32) for _ in range(2)])
        st1 = Rot([P.sb("st1", [128, 8], F32) for _ in range(4)])
        for s in range(NS):
            for i in range(seq_lens[s] // 128):
                tp = offs[s] + i * 128
                A = gA.next()
                dma("sp", A[:, 0, :], OFB[0][tp:tp + 128, :], [OFB[0]], [A], (A.key, 0))
                dma("sp", A[:, 1, :], OFB[1][tp:tp + 128, :], [OFB[1]], [A], (A.key, 1))
                dma("sp", A[:, 2, :], HG[tp:tp + 128, :], [HG], [A], (A.key, 2))
                tt("dve", A[:, 0, :], A[:, 0, :], A[:, 1, :], ALU.add, [A], [A])
                SQ = f512.next()
                S = st1.next()
                act(SQ[:], A[:, 0, :], AF.Square, [A], [SQ])
                op("dve", lambda e, S=S, SQ=SQ: e.tensor_reduce(out=S[:, 0:4], in_=SQ[:].rearrange("p (h d) -> p h d", d=128), axis=AX.X, op=ALU.add),
                   reads=[SQ], writes=[S])
                act(S[:, 0:4], S[:, 0:4], AF.Ln, [S], [S], scale=1.0 / 128, bias=EPS)
                act(S[:, 0:4], S[:, 0:4], AF.Exp, [S], [S], scale=-0.5)
                tt("dve", A[:, 0, :].rearrange("p (h d) -> p h d", d=128), A[:, 0, :].rearrange("p (h d) -> p h d", d=128),
                   S[:, 0:4].unsqueeze(2).broadcast_to([128, 4, 128]), ALU.mult, [A, S], [A])
                tt("pool", A[:, 0, :].rearrange("p (h d) -> p h d", d=128), A[:, 0, :].rearrange("p (h d) -> p h d", d=128),
                   HGG[:, l, :].unsqueeze(1).broadcast_to([128, 4, 128]), ALU.mult, [A, HGG], [A])
                O = gO.next()
                tt("dve", O[:], A[:, 0, :], A[:, 2, :], ALU.mult, [A], [O])
                dma("sp", MIX[tp:tp + 128, 512:1024], O[:], [O], [MIX], O.key)

    def phase_C1(l):
        begin_phase()
        common_pools()
        xs, hT = pl.xs, pl.hT
        wst = Rot([P.sb("wst", [128, 2048], F32) for _ in range(2)])
        mxs = Rot([P.sb("mxs", [128, 1024], BF16) for _ in range(2)])
        x1s = Rot([P.sb("x1s", [128, 1024], F32) for _ in range(2)])
        load_weight([(Wd["w_out"][l, c * 128:(c + 1) * 128, :], Wout(c, 0, 1024), None) for c in range(8)], wst)
        xsrc = x_in if l == 0 else XL
        for s in range(NS):
            for i in range(seq_lens[s] // 128):
                xg = xoffs[s] + i * 128
                tp = offs[s] + i * 128
                M = mxs.next()
                X = xs.next()
                X1s = x1s.next()
                HT = hT.next()
                dma("sp", M[:], MIX[tp:tp + 128, :], [MIX], [M], M.key)
                dma("sp", X[:], xsrc[xg:xg + 128, :], [xsrc], [X], X.key)
                for c in range(8):
                    tr(bfv(pT)[:, c * 128:(c + 1) * 128], M[:, c * 128:(c + 1) * 128], identb[:], [M, identb], [pT])
                cp("dve", HT[:, :, 0:128], bfv(pT).rearrange("p (c t) -> p c t", t=128), [pT], [HT])
                for nb in range(2):
                    ps = psr.next()
                    for c in range(8):
                        mm(ps[:], HT[:, c, 0:128], Wout(c, nb * 512, (nb + 1) * 512), c == 0, c == 7, [HT, RW], [ps])
                    tt("dve", X1s[:, nb * 512:(nb + 1) * 512], ps[:], X[:, nb * 512:(nb + 1) * 512], ALU.add, [ps, X], [X1s])
                dma("sp", X1[xg:xg + 128, :], X1s[:], [X1s], [X1], X1s.key)
                HT2 = hT.next()
                rmsnorm_to_hT(X1s, None, HT2)
                dma("sp", H2T[:, tp:tp + 128].rearrange("(c p) t -> p c t", p=128), HT2[:, :, 0:128], [HT2], [H2T], HT2.key)

    NH = NFG // 2
    TW = 510

    def phase_C2(l, p):
        begin_phase()
        xs = Rot([P.sb("xs", [128, 1024], F32) for _ in range(3)])
        hT5 = Rot([P.sb("hT5", [128, 8, 512], BF16) for _ in range(2)])
        wst = Rot([P.sb("wst", [128, 2048], F32) for _ in range(2)])
        actT = Rot([P.sb("actT", [128, NH, TW], BF16) for _ in range(2)])
        fC = Rot([P.sb("fC", [128, TW], F32) for _ in range(4)])
        fU = Rot([P.sb("fU", [128, TW], F32) for _ in range(4)])
        fE = Rot([P.sb("fE", [128, TW], F32) for _ in range(4)])
        fP = Rot([P.sb("fP", [128, TW], F32) for _ in range(4)])
        yo = Rot([P.sb("yo", [128, 1024], F32) for _ in range(2)])
        jobs = []
        for c in range(8):
            g = lng[:, l, 1, c:c + 1]
            jobs.append((Wd["w_up"][l, c * 128:(c + 1) * 128, p * 1408:(p + 1) * 1408], RW[:, c * 2816:c * 2816 + 1408], g))
            jobs.append((Wd["w_up"][l, c * 128:(c + 1) * 128, D_FF + p * 1408:D_FF + (p + 1) * 1408], RW[:, c * 2816 + 1408:(c + 1) * 2816], g))
        for j in range(NH):
            fg = p * NH + j
            jobs.append((Wd["w_down"][l, fg * 128:(fg + 1) * 128, :], Wdn(j, 0, 1024), None))
        load_weight(jobs, wst)
        xsrc = X1 if p == 0 else YP
        ydst = YP if p == 0 else (y_out if l == depth - 1 else XL)
        for s in range(NS):
            T = seq_lens[s]
            for t0 in range(0, T, TW):
                n = min(TW, T - t0)
                xg = xoffs[s] + t0
                tp = offs[s] + t0
                HT = hT5.next()
                dma("sp", HT[:, :, 0:n + 2], H2T[:, tp - 1:tp + n + 1].rearrange("(c p) t -> p c t", p=128), [H2T], [HT], HT.key)
                AT = actT.next()
                SQ_S = 0.044715 ** 0.5
                for g0 in range(0, NH, 3):
                    grp = list(range(g0, min(NH, g0 + 3)))
                    st = {}
                    for j in grp:
                        pg, pu = psr.next(), psr.next()
                        for c in range(8):
                            mm(pg[:, 0:n + 2], WupG(c, j), HT[:, c, 0:n + 2], c == 0, c == 7, [HT, RW], [pg])
                        for c in range(8):
                            mm(pu[:, 0:n], WupU(c, j), HT[:, c, 1:n + 1], c == 0, c == 7, [HT, RW], [pu])
                        st[j] = (pg, pu, fC.next(), fU.next(), fE.next(), fP.next())
                    w = lambda j, jj: CVW[:, l, jj, p * NH + j:p * NH + j + 1]
                    for j in grp:
                        pg, pu, C, U, E, Pc = st[j]
                        act(C[:, 0:n], pg[:, 0:n], AF.Identity, [pg, CVW], [C], scale=w(j, 0), bias=w(j, 3))
                    for j in grp:
                        pg, pu, C, U, E, Pc = st[j]
                        stt(C[:, 0:n], pg[:, 1:n + 1], w(j, 1), C[:, 0:n], ALU.mult, ALU.add, [pg, CVW, C], [C])
                    for j in grp:
                        pg, pu, C, U, E, Pc = st[j]
                        stt(C[:, 0:n], pg[:, 2:n + 2], w(j, 2), C[:, 0:n], ALU.mult, ALU.add, [pg, CVW, C], [C])
                    for j in grp:
                        pg, pu, C, U, E, Pc = st[j]
                        tt("dve", Pc[:, 0:n], pu[:, 0:n], C[:, 0:n], ALU.mult, [pu, C], [Pc])
                    for j in grp:
                        pg, pu, C, U, E, Pc = st[j]
                        act(U[:, 0:n], C[:, 0:n], AF.Square, [C], [U], scale=SQ_S)
                    for j in grp:
                        pg, pu, C, U, E, Pc = st[j]
                        stt(U[:, 0:n], U[:, 0:n], 1.0, C[:, 0:n], ALU.add, ALU.mult, [U, C], [U])
                    for j in grp:
                        pg, pu, C, U, E, Pc = st[j]
                        ts("pool", U[:, 0:n], U[:, 0:n], -25.0, None, ALU.max, None, [U], [U])
                    for j in grp:
                        pg, pu, C, U, E, Pc = st[j]
                        act(E[:, 0:n], U[:, 0:n], AF.Exp, [U], [E], scale=-GELU_C)
                    for j in grp:
                        pg, pu, C, U, E, Pc = st[j]
                        act(E[:, 0:n], E[:, 0:n], AF.Ln, [E], [E], bias=1.0)
                    for j in grp:
                        pg, pu, C, U, E, Pc = st[j]
                        act(E[:, 0:n], E[:, 0:n], AF.Exp, [E], [E], scale=-1.0)
                    for j in grp:
                        pg, pu, C, U, E, Pc = st[j]
                        tt("pool", AT[:, j, 0:n], Pc[:, 0:n], E[:, 0:n], ALU.mult, [Pc, E], [AT])
                for sub in range(0, n, 128):
                    m = min(128, n - sub)
                    X = xs.next()
                    Y = yo.next()
                    dma("sp", X[0:m, :], xsrc[xg + sub:xg + sub + m, :], [xsrc], [X], X.key)
                    for nb in range(2):
                        ps = psr.next()
                        for j in range(NH):
                            mm(ps[0:m, :], AT[:, j, sub:sub + m], Wdn(j, nb * 512, (nb + 1) * 512), j == 0, j == NH - 1, [AT, RW], [ps])
                        tt("dve", Y[0:m, nb * 512:(nb + 1) * 512], ps[0:m, :], X[0:m, nb * 512:(nb + 1) * 512], ALU.add, [ps, X], [Y])
                    dma("sp", ydst[xg + sub:xg + sub + m, :], Y[0:m, :], [Y], [ydst], Y.key)

    phases = [("A", phase_A), ("mA", mixer_A), ("mB", mixer_B), ("cB", combine_B), ("mC", mixer_C), ("cC", combine_C), ("C1", phase_C1), ("C2a", lambda l: phase_C2(l, 0)), ("C2b", lambda l: phase_C2(l, 1))]
    done = stop_after == (0, "setup")
    for l in range(depth if not done else 0):
        for name, fn in phases:
            fn(l)
            if stop_after == (l, name):
                done = True
                break
        if done:
            break
    P.emit()
    return nc, P


SEQ_LENS = (16384, 4096, 4096)
_CACHE = {}


def kernel(x_prompt, x_sample, **w):
    consts = host_consts()
    if "nc" not in _CACHE:
        _CACHE["nc"] = build(SEQ_LENS)[0]
    nc = _CACHE["nc"]
    xp = np.asarray(x_prompt, np.float32)
    xsm = np.asarray(x_sample, np.float32)
    wmap = {k: np.ascontiguousarray(np.asarray(w[k], np.float32)) for k in W_SHAPES}
    in_maps = []
    for c in range(8):
        cc = c % 2
        xcat = np.concatenate([xp[cc], xsm[2 * cc], xsm[2 * cc + 1]], axis=0)
        m = {"x": np.ascontiguousarray(xcat)}
        m.update(wmap)
        m.update(consts)
        in_maps.append(m)
    res = run_bass_kernel_spmd(nc, in_maps, core_ids=list(range(8)))
    yp = np.zeros_like(xp)
    ys = np.zeros_like(xsm)
    for cc in range(2):
        y = np.asarray(res.results[cc]["y"], np.float32)
        yp[cc] = y[0:16384]
        ys[2 * cc] = y[16384:20480]
        ys[2 * cc + 1] = y[20480:24576]
    return (yp, ys)
```
